# Optimizing a Trainium2 kernel written in Bass

```python
import math
import jax, jax.numpy as jnp
from jax import lax
import numpy as np

D_MODEL = 1024
BATCH = 8
SEQ = 2048
DEPTH = 2

N_META = 16
D_A = D_MODEL // 2
D_B = D_MODEL // 2
CONV_A_WIDTH = 31
POOL_WINDOWS = (2, 4, 8, 16)
N_POOL = len(POOL_WINDOWS)
POOL_C = D_B // N_POOL
D_IN_AB = 2 * D_A + D_B
D_C = D_MODEL
CONV_C_WIDTH = 3
D_IN_C = 3 * D_C
D_FF = 2816
CONV_F_WIDTH = 3
DEEPNORM_ALPHA = (2.0 * DEPTH) ** 0.25
DEEPNORM_BETA = (8.0 * DEPTH) ** -0.25
LN_EPS = 1e-5
N_EVEN = (DEPTH + 1) // 2
N_ODD = DEPTH // 2

kernel_name = "hybrid_conformer_pool_shortconv_encoder"


def layer_norm(x, g, b):
    x32 = x.astype(jnp.float32)
    mu = jnp.mean(x32, axis=-1, keepdims=True)
    var = jnp.mean(jnp.square(x32 - mu), axis=-1, keepdims=True)
    y = (x32 - mu) * lax.rsqrt(var + LN_EPS) * g.astype(jnp.float32) + b.astype(jnp.float32)
    return y.astype(x.dtype)


def dwconv(x, w, b):
    k = w.shape[0]
    pad = k // 2
    y = lax.conv_general_dilated(
        x, w[:, None, :].astype(x.dtype), window_strides=(1,), padding=[(pad, pad)],
        dimension_numbers=("NWC", "WIO", "NWC"), feature_group_count=x.shape[-1])
    return y + b


def multiscale_pool(u):
    bsz, length = u.shape[0], u.shape[1]
    u32 = u.astype(jnp.float32)
    csum = jnp.concatenate(
        [jnp.zeros((bsz, 1) + u.shape[2:], jnp.float32), jnp.cumsum(u32, axis=1)], axis=1)
    t = jnp.arange(length)
    outs = []
    for g, w in enumerate(POOL_WINDOWS):
        lo = jnp.clip(t - w // 2, 0, length)
        hi = jnp.clip(t + (w - w // 2), 0, length)
        s = jnp.take(csum[:, :, g], hi, axis=1) - jnp.take(csum[:, :, g], lo, axis=1)
        cnt = (hi - lo).astype(jnp.float32)
        outs.append(s / cnt[None, :, None] - u32[:, :, g])
    return jnp.stack(outs, axis=2).astype(u.dtype)


def mixer_ab(x, w_in, b_in, conv_w, conv_b, n_g, n_b, pool_w, pool_scale, w_out, b_out):
    bsz, length, _ = x.shape
    h = x @ w_in + b_in
    a_val, a_gate, u_b = jnp.split(h, [D_A, 2 * D_A], axis=-1)
    a = a_val * jax.nn.sigmoid(a_gate)
    a = dwconv(a, conv_w, conv_b)
    a = jax.nn.silu(layer_norm(a, n_g, n_b))
    p = multiscale_pool(u_b.reshape(bsz, length, N_POOL, POOL_C))
    p = jnp.einsum("blgc,gcd->blgd", p, pool_w).reshape(bsz, length, D_B) * pool_scale
    return jnp.concatenate([a, p], axis=-1) @ w_out + b_out


def mixer_c(x, w_in, b_in, conv_w, conv_b, w_out, b_out):
    h = x @ w_in + b_in
    bg, cg, v = jnp.split(h, 3, axis=-1)
    y = bg * dwconv(cg * v, conv_w, conv_b)
    return y @ w_out + b_out


def conv_glu(x, w_up, b_up, conv_w, conv_b, w_down, b_down):
    h = x @ w_up + b_up
    g, v = jnp.split(h, 2, axis=-1)
    g = dwconv(g, conv_w, conv_b)
    return (jax.nn.silu(g) * v) @ w_down + b_down


def setup_inputs(seed: int = 0) -> dict:
    key = jax.random.key(seed)
    ks = iter(jax.random.split(key, 40))
    nrm = lambda shape, s: jax.random.normal(next(ks), shape, jnp.float32) * s
    d = D_MODEL
    return {
        "x": nrm((BATCH, SEQ, d), 1.0),
        "meta_tokens": nrm((N_META, d), 1.0),
        "w_in_ab": nrm((N_EVEN, d, D_IN_AB), d ** -0.5),
        "b_in_ab": nrm((N_EVEN, D_IN_AB), 0.02),
        "conv_a_w": nrm((N_EVEN, CONV_A_WIDTH, D_A), CONV_A_WIDTH ** -0.5),
        "conv_a_b": nrm((N_EVEN, D_A), 0.02),
        "norm_a_g": 1.0 + nrm((N_EVEN, D_A), 0.05),
        "norm_a_b": nrm((N_EVEN, D_A), 0.02),
        "pool_w": nrm((N_EVEN, N_POOL, POOL_C, POOL_C), POOL_C ** -0.5),
        "pool_scale": 1.0 + nrm((N_EVEN, D_B), 0.1),
        "w_out_ab": nrm((N_EVEN, D_A + D_B, d), (D_A + D_B) ** -0.5 * DEEPNORM_BETA),
        "b_out_ab": nrm((N_EVEN, d), 0.02),
        "w_in_c": nrm((N_ODD, d, D_IN_C), d ** -0.5),
        "b_in_c": nrm((N_ODD, D_IN_C), 0.02),
        "conv_c_w": nrm((N_ODD, CONV_C_WIDTH, D_C), CONV_C_WIDTH ** -0.5),
        "conv_c_b": nrm((N_ODD, D_C), 0.02),
        "w_out_c": nrm((N_ODD, D_C, d), D_C ** -0.5 * DEEPNORM_BETA),
        "b_out_c": nrm((N_ODD, d), 0.02),
        "mix_ln_g": 1.0 + nrm((DEPTH, d), 0.05),
        "mix_ln_b": nrm((DEPTH, d), 0.02),
        "ffn_w_up": nrm((DEPTH, d, 2 * D_FF), d ** -0.5),
        "ffn_b_up": nrm((DEPTH, 2 * D_FF), 0.02),
        "ffn_conv_w": nrm((DEPTH, CONV_F_WIDTH, D_FF), CONV_F_WIDTH ** -0.5),
        "ffn_conv_b": nrm((DEPTH, D_FF), 0.02),
        "ffn_w_down": nrm((DEPTH, D_FF, d), D_FF ** -0.5 * DEEPNORM_BETA),
        "ffn_b_down": nrm((DEPTH, d), 0.02),
        "ffn_ln_g": 1.0 + nrm((DEPTH, d), 0.05),
        "ffn_ln_b": nrm((DEPTH, d), 0.02),
    }


def reference(x, meta_tokens, w_in_ab, b_in_ab, conv_a_w, conv_a_b, norm_a_g, norm_a_b,
              pool_w, pool_scale, w_out_ab, b_out_ab, w_in_c, b_in_c, conv_c_w, conv_c_b,
              w_out_c, b_out_c, mix_ln_g, mix_ln_b, ffn_w_up, ffn_b_up, ffn_conv_w,
              ffn_conv_b, ffn_w_down, ffn_b_down, ffn_ln_g, ffn_ln_b):
    bsz = x.shape[0]
    meta = jnp.broadcast_to(meta_tokens[None].astype(x.dtype), (bsz, N_META, x.shape[-1]))
    h = jnp.concatenate([meta, x], axis=1)
    for i in range(DEPTH):
        if i % 2 == 0:
            j = i // 2
            m = mixer_ab(h, w_in_ab[j], b_in_ab[j], conv_a_w[j], conv_a_b[j], norm_a_g[j],
                         norm_a_b[j], pool_w[j], pool_scale[j], w_out_ab[j], b_out_ab[j])
        else:
            j = i // 2
            m = mixer_c(h, w_in_c[j], b_in_c[j], conv_c_w[j], conv_c_b[j], w_out_c[j], b_out_c[j])
        h = layer_norm(DEEPNORM_ALPHA * h + m, mix_ln_g[i], mix_ln_b[i])
        f = conv_glu(h, ffn_w_up[i], ffn_b_up[i], ffn_conv_w[i], ffn_conv_b[i],
                     ffn_w_down[i], ffn_b_down[i])
        h = layer_norm(DEEPNORM_ALPHA * h + f, ffn_ln_g[i], ffn_ln_b[i])
    return h[:, N_META:]
```

```python
import sys
import numpy as np
from contextlib import ExitStack
import concourse.bass as bass
import concourse.mybir as mybir
from concourse.bass_utils import run_bass_kernel_spmd

F32 = mybir.dt.float32
BF16 = mybir.dt.bfloat16
AF = mybir.ActivationFunctionType
ALU = mybir.AluOpType

D = 1024
NCH = 8
SEQ = 2048
NMETA = 16
T = SEQ + NMETA
TT = 344
NT = 6
HT = 3 * TT
DFF = 2816
NFC = 22
ALPHA = 4.0 ** 0.25
EPS = 1e-5
HALO = 15

PVECS = [
    ("b_in_ab", 12), ("conv_a_w", 31 * 4), ("conv_a_b", 4), ("norm_a_g", 4), ("norm_a_b", 4),
    ("pool_scale", 4), ("b_out_ab", 8), ("b_in_c", 24), ("conv_c_w", 3 * 8), ("conv_c_b", 8),
    ("b_out_c", 8), ("mix_ln_g", 16), ("mix_ln_b", 16), ("ffn_b_up", 88), ("ffn_conv_w", 2 * 3 * 22),
    ("ffn_conv_b", 44), ("ffn_b_down", 16), ("ffn_ln_g", 16), ("ffn_ln_b", 16),
]
PCOL = {}
_c = 0
for _n, _k in PVECS:
    PCOL[_n] = _c
    _c += _k
NPCOL = _c

DCOL = {"hbg": 0, "hbv": 4, "ag": 8, "cc": 8 + 32, "eps": 8 + 64}
NDCOL = 8 + 64 + 1

GA_M0IN, GA_M0OUT, GA_F0UP, GA_M1IN, GA_M1OUT, GA_F1UP = 0, 6, 10, 32, 44, 48
NGA = 70


def _vec_cols(v):
    v = np.asarray(v, np.float32).reshape(-1, 128)
    return np.ascontiguousarray(v.T)


def _group_cols(W, cols):
    sub = W[:, cols]
    return np.ascontiguousarray(sub.reshape(8, 128, -1).transpose(1, 0, 2))


def _rng(a, n=128):
    return list(range(a, a + n))


def host_layout(inp):
    f = lambda k: np.asarray(inp[k], np.float32)
    pt = np.concatenate([
        _vec_cols(f("b_in_ab")[0]), _vec_cols(f("conv_a_w")[0]), _vec_cols(f("conv_a_b")[0]),
        _vec_cols(f("norm_a_g")[0]), _vec_cols(f("norm_a_b")[0]), _vec_cols(f("pool_scale")[0]),
        _vec_cols(f("b_out_ab")[0]), _vec_cols(f("b_in_c")[0]), _vec_cols(f("conv_c_w")[0]),
        _vec_cols(f("conv_c_b")[0]), _vec_cols(f("b_out_c")[0]), _vec_cols(f("mix_ln_g")),
        _vec_cols(f("mix_ln_b")), _vec_cols(f("ffn_b_up")), _vec_cols(f("ffn_conv_w")),
        _vec_cols(f("ffn_conv_b")), _vec_cols(f("ffn_b_down")), _vec_cols(f("ffn_ln_g")),
        _vec_cols(f("ffn_ln_b")),
    ], axis=1)
    assert pt.shape == (128, NPCOL)
    groups = []
    w = f("w_in_ab")[0]
    for j in range(4):
        groups.append(_group_cols(w, _rng(512 + j * 128) + _rng(j * 128)))
    for i in range(2):
        groups.append(_group_cols(w, _rng(1024 + i * 256, 256)))
    w = f("w_out_ab")[0]
    for i in range(4):
        groups.append(_group_cols(w, _rng(i * 256, 256)))
    w = f("ffn_w_up")[0]
    for j in range(22):
        groups.append(_group_cols(w, _rng(j * 128) + _rng(DFF + j * 128)))
    w = f("w_in_c")[0]
    for j in range(8):
        groups.append(_group_cols(w, _rng(1024 + j * 128) + _rng(2048 + j * 128)))
    for i in range(4):
        groups.append(_group_cols(w, _rng(i * 256, 256)))
    w = f("w_out_c")[0]
    for i in range(4):
        groups.append(_group_cols(w, _rng(i * 256, 256)))
    w = f("ffn_w_up")[1]
    for j in range(22):
        groups.append(_group_cols(w, _rng(j * 128) + _rng(DFF + j * 128)))
    wA = np.stack(groups, 0)
    assert wA.shape[0] == NGA
    slabs = []
    for l in range(2):
        w = f("ffn_w_down")[l]
        for oc in range(8):
            sub = w[:, oc * 128:(oc + 1) * 128]
            slabs.append(np.ascontiguousarray(sub.reshape(22, 128, 128).transpose(1, 0, 2)))
    wB = np.stack(slabs, 0)
    poolw = np.ascontiguousarray(f("pool_w")[0].transpose(1, 0, 2))
    ctab = np.zeros((128, 256), np.float32)
    ctab[:, :128] = np.eye(128, dtype=np.float32)
    for g in range(4):
        wd = 2 ** (g + 1)
        base = 128 + g * 32
        for t in range(8):
            cnt = min(t + wd // 2, wd) if t < wd // 2 else wd
            cnt = t + wd // 2 if t < wd // 2 else wd
            ctab[:, base + t] = 1.0 / cnt
            ctab[:, base + 8 + t] = (wd - cnt) / cnt
        for i in range(8):
            t = T - 8 + i
            cnt = (T - t + wd // 2) if t > T - wd // 2 else wd
            ctab[:, base + 16 + i] = 1.0 / cnt
            ctab[:, base + 24 + i] = (wd - cnt) / cnt
    x = f("x")
    meta = f("meta_tokens")
    xts = []
    for b in range(x.shape[0]):
        h0 = np.concatenate([meta, x[b]], axis=0)
        xts.append(np.ascontiguousarray(h0.T.reshape(8, 128, T).transpose(1, 0, 2)))
    return dict(pt=pt, wA=wA, wB=wB, poolw=poolw, ctab=ctab), xts


class Op:
    __slots__ = ("eng", "fn", "reads", "writes", "dma", "deps", "signal", "sigval", "waits", "tag", "pos")

    def __init__(self, eng, fn, reads, writes, dma):
        self.eng, self.fn, self.reads, self.writes, self.dma = eng, fn, tuple(reads), tuple(writes), dma
        self.deps = []
        self.signal = False
        self.sigval = 0
        self.waits = []


class Sched:
    ENGS = ("pe", "act", "dve", "pool", "sp")

    def __init__(self):
        self.ops = []
        self.pending = []
        self.in_deferred = False

    def op(self, eng, fn, reads=(), writes=(), dma=None):
        touched = set(reads) | set(writes)
        if self.pending and not self.in_deferred:
            last = -1
            for i, it in enumerate(self.pending):
                if it[1] & touched:
                    last = i
            if last >= 0:
                items = self.pending[:last + 1]
                del self.pending[:last + 1]
                self._run(items)
        o = Op(eng, fn, reads, writes, dma)
        o.tag = sys._getframe(1).f_lineno
        self.ops.append(o)
        return o

    def _run(self, items):
        assert not self.in_deferred
        self.in_deferred = True
        try:
            for it in items:
                it[2]()
        finally:
            self.in_deferred = False

    def defer(self, countdown, writes, fn):
        assert not self.in_deferred
        self.pending.append([countdown, set(writes), fn])

    def pe_unit(self):
        if self.in_deferred or not self.pending:
            return
        for it in self.pending:
            it[0] -= 1
        due = [i for i, it in enumerate(self.pending) if it[0] <= 0]
        if not due:
            return
        last = max(due)
        items = self.pending[:last + 1]
        del self.pending[:last + 1]
        self._run(items)

    def flush(self):
        items = self.pending[:]
        del self.pending[:]
        self._run(items)

    def analyze(self):
        last_w = {}
        readers = {}
        for i, o in enumerate(self.ops):
            o.pos = i
        for o in self.ops:
            deps = []
            o.writes = o.writes + tuple(r for r in o.reads if isinstance(r, tuple) and r[0] == "ps" and r not in o.writes)
            for r in o.reads:
                w = last_w.get(r)
                if w is not None:
                    deps.append(w)
            for r in o.writes:
                w = last_w.get(r)
                if w is not None:
                    deps.append(w)
                deps.extend(readers.get(r, ()))
            seen = set()
            latest = {}
            for d in deps:
                if d is o or id(d) in seen:
                    continue
                seen.add(id(d))
                if d.dma is None and d.eng == "pe" and o.eng == "pe":
                    continue
                if d.dma is not None:
                    o.deps.append(d)
                else:
                    cur = latest.get(d.eng)
                    if cur is None or d.pos > cur.pos:
                        latest[d.eng] = d
            for d in latest.values():
                o.deps.append(d)
                d.signal = True
            for r in o.reads:
                if r not in o.writes:
                    readers.setdefault(r, []).append(o)
            for r in o.writes:
                last_w[r] = o
                readers[r] = []
        cnt = {}
        for o in self.ops:
            if o.dma is not None:
                cnt[o.dma] = cnt.get(o.dma, 0) + 16
                o.sigval = cnt[o.dma]
            elif o.signal:
                cnt[o.eng] = cnt.get(o.eng, 0) + 1
                o.sigval = cnt[o.eng]
        waited = {e: {} for e in self.ENGS}
        for o in self.ops:
            need = {}
            for d in o.deps:
                key = d.dma if d.dma is not None else d.eng
                if d.sigval > need.get(key, 0):
                    need[key] = d.sigval
            wd = waited[o.eng]
            for key, v in need.items():
                if v > wd.get(key, 0):
                    wd[key] = v
                    o.waits.append((key, v))
        return cnt

    def sem_keys(self):
        keys = []
        for o in self.ops:
            k = o.dma if o.dma is not None else (o.eng if o.signal else None)
            if k is not None and k not in keys:
                keys.append(k)
        return keys

    def emit(self, eng, handle, sems):
        for o in self.ops:
            if o.eng != eng:
                continue
            for key, v in o.waits:
                handle.wait_ge(sems[key], v)
            inst = o.fn(handle)
            if o.dma is not None:
                inst.then_inc(sems[o.dma], 16)
            elif o.signal:
                inst.then_inc(sems[eng], 1)


def tile_ranges(lo, hi, n):
    step = (hi - lo + n - 1) // n
    out = []
    a = lo
    while a < hi:
        b = min(hi, a + step)
        out.append((a, b))
        a = b
    return out


def build(n_stages=4, max_ops=None):
    nc = bass.Bass("TRN2", target_bir_lowering=False)
    S = Sched()
    es = ExitStack()

    def dram(name, shape, kind):
        return nc.dram_tensor(name, shape, F32, kind=kind).ap()

    xT = dram("xT", [128, NCH, T], "ExternalInput")
    pt_d = dram("pt", [128, NPCOL], "ExternalInput")
    wA = dram("wA", [NGA, 128, 8, 256], "ExternalInput")
    wB = dram("wB", [16, 128, NFC, 128], "ExternalInput")
    poolw_d = dram("poolw", [128, 4, 128], "ExternalInput")
    ctab_d = dram("ctab", [128, 256], "ExternalInput")
    outT = dram("outT", [128, NCH, SEQ], "ExternalOutput")

    def sb(name, shape, dt):
        return es.enter_context(nc.sbuf_tensor(name, shape, dt))

    hb = sb("hb", [128, NCH, T], BF16)
    hr = sb("hr", [128, NCH, T], F32)
    ARENA_E = NFC * HT
    arena = sb("arena", [128, ARENA_E], BF16)
    ringA = sb("ringA", [128, 4, 8, 256], BF16)
    RB_SLOTS = 3
    ringB = sb("ringB", [128, RB_SLOTS, NFC, 128], BF16)
    PT = sb("PT", [128, NPCOL], F32)
    DT = sb("DT", [128, NDCOL], F32)
    CT = sb("CT", [128, 128], F32)
    ident = sb("ident", [128, 128], BF16)
    negI = sb("negI", [128, 4, 128], BF16)
    ones8 = sb("ones8", [128, 128], BF16)
    ones4 = sb("ones4", [128, 128], BF16)
    poolw = sb("poolw_sb", [128, 4, 128], BF16)
    diag = sb("diag", [128, 6, 128], BF16)
    hsave = sb("hsave", [128, NCH, HALO], BF16)
    zz = sb("zz", [128, 2, NCH, TT], BF16)
    zb = zz[:, 0]
    zsq = zz[:, 1]
    ZZ_KEYS = [("zb", c) for c in range(NCH)] + [("zsq", c) for c in range(NCH)]
    st = sb("st", [128, 1, 3, TT], F32)
    NTF = 4
    tf = sb("tf", [128, NTF, 352], F32)
    NGB = 3
    gbuf = sb("gbuf", [128, NGB, 352], BF16)
    pbuf = sb("pbuf", [128, 2, TT], BF16)
    etmp = sb("etmp", [128, 2, 8], F32)
    mz = sb("mz", [128, NCH, 2], F32)
    mzb = sb("mzb", [128, NCH, 2], BF16)
    mzsq = sb("mzsq", [128, NCH, 2], BF16)
    mst = sb("mst", [128, 4], F32)
    hnext = sb("hnext", [128, NCH, 2], BF16)
    ps = [es.enter_context(nc.psum_tensor(f"ps{i}", [128, 512], F32)) for i in range(8)]

    def aview(off, c, w):
        return arena[:, off:off + c * w].rearrange("p (c w) -> p c w", c=c)

    AW = HT + 2 * HALO
    abuf = aview(0, 4, AW)
    ubuf = aview(4 * AW, 4, AW)
    cat = aview(8 * AW, 8, HT)
    d31b = arena[:, 8 * AW + 8 * HT: 8 * AW + 8 * HT + 31 * 128].rearrange("p (k m) -> p k m", k=31)
    d31a = arena[:, 8 * AW: 8 * AW + 31 * 128].rearrange("p (k m) -> p k m", k=31)
    assert 8 * AW + 8 * HT + 31 * 128 <= ARENA_E
    CVW = HT + 2
    cvb = aview(0, 8, CVW)
    ybuf = aview(8 * CVW, 8, HT)
    hid = aview(0, NFC, HT)
    ac32 = ringB[:, :, :, :].rearrange("p s k m -> p (s k m)")[:, 0:2 * 4 * HT].bitcast(F32) \
        .rearrange("p (c w) -> p c w", c=4)
    assert 2 * 4 * HT <= RB_SLOTS * NFC * 128
    RB_KEYS = [("rB", s) for s in range(RB_SLOTS)]
    zz2 = arena[:, 0:2 * NCH * TT].rearrange("p (a c w) -> p a c w", a=2, c=NCH)
    zz4 = arena[:, 2 * NCH * TT:4 * NCH * TT].rearrange("p (a c w) -> p a c w", a=2, c=NCH)
    zz3 = arena[:, 8 * AW + 8 * HT: 8 * AW + 8 * HT + 2 * 4 * TT].rearrange("p (a c w) -> p a c w", a=2, c=4)
    HID_H1_KEYS = [("hid", kc, t) for kc in range(NFC) for t in (3, 4, 5)]

    def pcol(name, idx):
        c = PCOL[name] + idx
        return PT[:, c:c + 1]

    def dcol(name, idx=0):
        c = DCOL[name] + idx
        return DT[:, c:c + 1]

    state = {"bank": 0, "tf": 0, "gb": 0, "pb": 0, "st": 0, "et": 0}

    NGBANK = 6

    def bank():
        b = state["bank"]
        state["bank"] = (b + 1) % NGBANK
        return ps[b], ("ps", b)

    def tfslot():
        i = state["tf"]
        state["tf"] = (i + 1) % NTF
        return tf[:, i, :], ("tf", i)

    def gbslot():
        i = state["gb"]
        state["gb"] = (i + 1) % NGB
        return gbuf[:, i, :], ("gb", i)

    def pbslot():
        i = state["pb"]
        state["pb"] = (i + 1) % 2
        return pbuf[:, i, :], ("pb", i)

    ring = {"a_next": 0, "b_next": 0}
    a_order = []
    b_order = []

    def plan_stage_groups():
        for h in range(2):
            a_order.extend(range(GA_M0IN, GA_M0IN + 6))
        for h in range(2):
            a_order.extend(range(GA_M0OUT, GA_M0OUT + 4))
        for h in range(2):
            a_order.extend(range(GA_F0UP, GA_F0UP + 22))
            b_order.extend(range(0, 8))
        for h in range(2):
            a_order.extend(range(GA_M1IN, GA_M1IN + 12))
            a_order.extend(range(GA_M1OUT, GA_M1OUT + 4))
        for h in range(2):
            a_order.extend(range(GA_F1UP, GA_F1UP + 22))
            b_order.extend(range(8, 16))

    plan_stage_groups()
    a_use = {"i": 0}
    b_use = {"i": 0}

    def a_prefetch(upto):
        while ring["a_next"] <= min(upto, len(a_order) - 1):
            n = ring["a_next"]
            s = n % 4
            g = a_order[n]
            S.op("pool", lambda e, s=s, g=g: e.dma_start(out=ringA[:, s], in_=wA[g], max_dma_last_dim=4096),
                 writes=[("rA", s)], dma=("dA", s))
            ring["a_next"] += 1

    def a_group(expect):
        n = a_use["i"]
        assert a_order[n] == expect, (n, a_order[n], expect)
        a_use["i"] += 1
        a_prefetch(n + 3)
        s = n % 4
        return ringA[:, s], ("rA", s)

    def b_prefetch(upto):
        while ring["b_next"] <= min(upto, len(b_order) - 1):
            n = ring["b_next"]
            s = n % RB_SLOTS
            g = b_order[n]
            S.op("pool", lambda e, s=s, g=g: e.dma_start(out=ringB[:, s], in_=wB[g], max_dma_last_dim=4096),
                 writes=[("rB", s)], dma=("dB", s))
            ring["b_next"] += 1

    def b_slab(expect):
        n = b_use["i"]
        assert b_order[n] == expect
        b_use["i"] += 1
        b_prefetch(n + RB_SLOTS - 1)
        s = n % RB_SLOTS
        return ringB[:, s], ("rB", s)

    def tile_of(tok):
        return min(tok // TT, NT - 1)

    def hb_keys(a, b, use_save=False):
        if use_save and a < HT:
            a = HT
        ks = [("hb", t) for t in range(tile_of(a), tile_of(b - 1) + 1)]
        if a < HT + HALO:
            ks += [("hbc", c, 0) for c in range(NCH)]
        if b > HT + HALO:
            ks += [("hbc", c, 1) for c in range(NCH)]
        return ks

    def mm_hb(pe, bk, wslot, c0, a, b, use_save, use_next=False):
        parts = []
        if use_next:
            assert b == HT + 1 and a < HT
            parts.append((hb, a, HT, 0, HT - a))
            parts.append((hnext, 0, 1, HT - a, HT - a + 1))
        elif use_save and a < HT:
            parts.append((hsave, a - (HT - HALO), HT - (HT - HALO), 0, HT - a))
            parts.append((hb, HT, b, HT - a, b - a))
        else:
            parts.append((hb, a, b, 0, b - a))
        inst = None
        for (src, sa, sbnd, o0, o1) in parts:
            for kc in range(8):
                inst = pe.matmul(bk[:, o0:o1], wslot[:, kc, c0:c0 + 128], src[:, kc, sa:sbnd],
                                 start=(kc == 0), stop=(kc == 7))
        return inst

    def mm_generic(pe, bk, n, wslot, c0, src, nk, sa):
        inst = None
        for kc in range(nk):
            inst = pe.matmul(bk[:, 0:n], wslot[:, kc, c0:c0 + 128], src[:, kc, sa:sa + n],
                             start=(kc == 0), stop=(kc == nk - 1))
        return inst

    def build_diag(dst, k, colap, keys):
        S.op("dve", lambda e: e.tensor_scalar(out=dst[:, k, :], in0=ident[:, :], scalar1=colap, scalar2=None, op0=ALU.mult),
             reads=["ident", "PT"], writes=list(keys))

    def ln_prelude(it):
        src, keys, nch, n = it["src"], it["keys"], it["nch"], it["n"]
        zo = 4 if it.get("alt") else 0
        if it.get("alt") == 9:
            for c in range(nch):
                S.op("act", lambda e, c=c: e.activation(out=zz4[:, 1, c, 0:n], in_=src(c), func=AF.Square),
                     reads=[keys[c]], writes=[("zz4q", c)] + HID_H1_KEYS)
                S.op("dve", lambda e, c=c: e.tensor_copy(out=zz4[:, 0, c, 0:n], in_=src(c)),
                     reads=[keys[c]], writes=[("zz4b", c)] + HID_H1_KEYS)
            return
        if it.get("alt") == 3:
            for c in range(nch):
                S.op("act", lambda e, c=c: e.activation(out=zz3[:, 1, c, 0:n], in_=src(c), func=AF.Square),
                     reads=[keys[c]] + it.get("xr", []), writes=[("zz3q", c), ("d31b",)])
                S.op("dve", lambda e, c=c: e.tensor_copy(out=zz3[:, 0, c, 0:n], in_=src(c)),
                     reads=[keys[c]] + it.get("xr", []), writes=[("zz3b", c), ("d31b",)])
            return
        if it.get("alt") == 8:
            for c in range(nch):
                S.op("act", lambda e, c=c: e.activation(out=zz2[:, 1, c, 0:n], in_=src(c), func=AF.Square),
                     reads=[keys[c]], writes=[("zz2q", c)] + HID_H1_KEYS)
                S.op("dve", lambda e, c=c: e.tensor_copy(out=zz2[:, 0, c, 0:n], in_=src(c)),
                     reads=[keys[c]], writes=[("zz2b", c)] + HID_H1_KEYS)
            return
        for f in ln_prelude_ops(it):
            f()

    def ln_prelude_ops(it):
        src, keys, nch, n = it["src"], it["keys"], it["nch"], it["n"]
        zo = 4 if it.get("alt") else 0
        assert it.get("alt") != 8

        def one(c):
            S.op("act", lambda e: e.activation(out=zsq[:, zo + c, 0:n], in_=src(c), func=AF.Square),
                 reads=[keys[c]] + it.get("xr", []), writes=[("zsq", zo + c)])
            S.op("dve", lambda e: e.tensor_copy(out=zb[:, zo + c, 0:n], in_=src(c)),
                 reads=[keys[c]] + it.get("xr", []), writes=[("zb", zo + c)])
        return [(lambda c=c: one(c)) for c in range(nch)]

    def ln_pieces(it, split=False):
        src, keys, nch, n, ones, epilogue = it["src"], it["keys"], it["nch"], it["n"], it["ones"], it["epi"]
        xr = it.get("xr", [])
        zbv, zqv, zbk, zqk = zb, zsq, "zb", "zsq"
        if it.get("alt"):
            if it.get("alt") == 8:
                zo = 0
                zbv, zqv, zbk, zqk = zz2[:, 0], zz2[:, 1], "zz2b", "zz2q"
            elif it.get("alt") == 9:
                zo = 0
            else:
                assert nch <= 4
                zo = 4
            mean_sb, st_a = tf[:, 0, 0:n], tf[:, 1, 0:n]
            km_, ka_ = ("tf", 0), ("tf", 1)
            bm, km = ps[4], ("ps", 4)
            be, ke = ps[5], ("ps", 5)
        else:
            zo = 0
            mean_sb, st_a = st[:, 0, 0, 0:n], st[:, 0, 1, 0:n]
            km_, ka_ = ("st", "m"), ("st", "a")
            bm, km = ps[6], ("ps", 6)
            be, ke = ps[7], ("ps", 7)

        if it.get("alt") in (3, 9):
            zo = 0
            if it.get("alt") == 3:
                zbv, zqv, zbk, zqk = zz3[:, 0], zz3[:, 1], "zz3b", "zz3q"
            else:
                zbv, zqv, zbk, zqk = zz4[:, 0], zz4[:, 1], "zz4b", "zz4q"
            mean_sb, st_a = tf[:, 2, 0:n], tf[:, 3, 0:n]
            km_, ka_ = ("tf", 2), ("tf", 3)
            bm, km = ps[2], ("ps", 2)
            be, ke = ps[3], ("ps", 3)

        def p0a():
            def stats(pe):
                inst = None
                for c in range(nch):
                    inst = pe.matmul(bm[:, 0:n], ones[:, :], zbv[:, zo + c, 0:n], start=(c == 0), stop=(c == nch - 1))
                for c in range(nch):
                    inst = pe.matmul(be[:, 0:n], ones[:, :], zqv[:, zo + c, 0:n], start=(c == 0), stop=(c == nch - 1))
                return inst
            S.op("pe", stats, reads=[(zbk, zo + c) for c in range(nch)] + [(zqk, zo + c) for c in range(nch)] + ["ones"],
                 writes=[km, ke])
            S.op("act", lambda e: e.activation(out=mean_sb, in_=bm[:, 0:n], func=AF.Copy), reads=[km], writes=[km_])
            S.op("act", lambda e: e.activation(out=st_a, in_=bm[:, 0:n], func=AF.Square), reads=[km], writes=[ka_])
            S.op("dve", lambda e: e.tensor_tensor(out=st_a, in0=be[:, 0:n], in1=st_a, op=ALU.subtract),
                 reads=[ke, ka_], writes=[ka_])

        def p0b():
            S.op("act", lambda e: e.activation(out=st_a, in_=st_a, func=AF.Sqrt, bias=dcol("eps"), scale=1.0),
                 reads=[ka_, "DT"], writes=[ka_])

        def p0c():
            S.op("dve", lambda e: e.reciprocal(out=bm[:, 0:n], in_=st_a), reads=[ka_], writes=[km])
            S.op("dve", lambda e: e.scalar_tensor_tensor(out=be[:, 0:n], in0=mean_sb, scalar=-1.0, in1=bm[:, 0:n],
                                                          op0=ALU.mult, op1=ALU.mult),
                 reads=[km_, km], writes=[ke])

        def piece0():
            p0a()
            p0b()
            p0c()

        def stA(c):
            S.op("dve", lambda e: e.tensor_tensor(out=src(c), in0=src(c), in1=bm[:, 0:n], op=ALU.mult),
                 reads=[keys[c], km] + xr, writes=[keys[c]])

        def stB(c):
            S.op("dve", lambda e: e.tensor_tensor(out=src(c), in0=src(c), in1=be[:, 0:n], op=ALU.add),
                 reads=[keys[c], ke] + xr, writes=[keys[c]])

        def stC(c):
            epilogue(c)
            if c == nch - 1 and it.get("after"):
                it["after"]()

        def chunk(c):
            stA(c)
            stB(c)
            stC(c)
        if split:
            piece0.parts = [p0a, p0b, p0c]
            return piece0, [(lambda c=c: stA(c)) for c in range(nch)], [(lambda c=c: stB(c)) for c in range(nch)], \
                [(lambda c=c: stC(c)) for c in range(nch)]
        return [piece0] + [(lambda c=c: chunk(c)) for c in range(nch)]

    def ln_part2(it):
        for p in ln_pieces(it):
            p()

    def ln_chain(items, base=2, per=13, split0=True):
        ln_prelude(items[0])
        for i in range(len(items)):
            it = items[i]
            nch = it["nch"]
            touched = set(ZZ_KEYS) | set(it["keys"]) | set(it["outs"])
            if i + 1 < len(items):
                touched |= set(items[i + 1]["keys"])
            p0, A, B, C = ln_pieces(it, split=True)
            steps = [[f] for f in p0.parts] if split0 else [[p0]]
            n0 = len(steps)
            for k in range(nch + 2):
                ops = []
                if k < nch:
                    ops.append(A[k])
                if 0 <= k - 1 < nch:
                    ops.append(B[k - 1])
                if 0 <= k - 2 < nch:
                    ops.append(C[k - 2])
                steps.append(ops)
            if i + 1 < len(items):
                for jn, f in enumerate(ln_prelude_ops(items[i + 1])):
                    steps[n0 + jn // 2].append(f)
            assert len(steps) <= per
            for k, ops in enumerate(steps):
                def run(ops=ops):
                    for f in ops:
                        f()
                S.defer(base + per * i + k, touched, run)

    def ln_pair(A, B, C):
        def skew(sets, n):
            for k in range(n + 2):
                for (_, sa, sb_, sc) in sets:
                    if k < n:
                        sa[k]()
                for (_, sa, sb_, sc) in sets:
                    if 0 <= k - 1 < n:
                        sb_[k - 1]()
                for (_, sa, sb_, sc) in sets:
                    if 0 <= k - 2 < n:
                        sc[k - 2]()
        ln_prelude(A)
        ln_prelude(B)
        pa = ln_pieces(A, split=True)
        pb_ = ln_pieces(B, split=True)
        pa[0]()
        pb_[0]()
        ln_prelude(C)
        skew([pa, pb_], A["nch"])
        pc = ln_pieces(C, split=True)
        pc[0]()
        skew([pc], C["nch"])

    def ln_triple(A, B, C):
        its = (A, B, C)
        for it in its:
            ln_prelude(it)
        sets = [ln_pieces(it, split=True) for it in its]
        for parts in zip(*[p[0].parts for p in sets]):
            for f in parts:
                f()
        n = A["nch"]
        for k in range(n + 2):
            for (_, sa, sb_, sc) in sets:
                if k < n:
                    sa[k]()
            for (_, sa, sb_, sc) in sets:
                if 0 <= k - 1 < n:
                    sb_[k - 1]()
            for (_, sa, sb_, sc) in sets:
                if 0 <= k - 2 < n:
                    sc[k - 2]()

    def mini_ln(ln_idx):
        if ln_idx in (0, 2):
            gname, bname, li = "mix_ln_g", "mix_ln_b", ln_idx // 2
        else:
            gname, bname, li = "ffn_ln_g", "ffn_ln_b", ln_idx // 2
        g0, b0 = PCOL[gname] + li * 8, PCOL[bname] + li * 8
        b4, k4 = ps[4], ("ps", 4)
        b5, k5 = ps[5], ("ps", 5)
        S.op("dve", lambda e: e.tensor_copy(out=mz[:, :, 0:1], in_=hr[:, :, HT:HT + 1]),
             reads=[("hr", c, 3) for c in range(NCH)], writes=["mz"])
        S.op("act", lambda e: e.activation(out=mzsq[:, :, 0:1], in_=mz[:, :, 0:1], func=AF.Square), reads=["mz"], writes=["mzsq"])
        S.op("dve", lambda e: e.tensor_copy(out=mzb[:, :, 0:1], in_=mz[:, :, 0:1]), reads=["mz"], writes=["mzb"])

        def stats(pe):
            inst = None
            for c in range(NCH):
                inst = pe.matmul(b4[:, 0:1], ones8[:, :], mzb[:, c, 0:1], start=(c == 0), stop=(c == NCH - 1))
            for c in range(NCH):
                inst = pe.matmul(b5[:, 0:1], ones8[:, :], mzsq[:, c, 0:1], start=(c == 0), stop=(c == NCH - 1))
            return inst
        S.op("pe", stats, reads=["mzb", "mzsq", "ones"], writes=[k4, k5])
        S.op("act", lambda e: e.activation(out=mst[:, 0:1], in_=b4[:, 0:1], func=AF.Copy), reads=[k4], writes=["mst0"])
        S.op("act", lambda e: e.activation(out=mst[:, 1:2], in_=b4[:, 0:1], func=AF.Square), reads=[k4], writes=["mst1"])
        S.op("dve", lambda e: e.tensor_tensor(out=mst[:, 1:2], in0=b5[:, 0:1], in1=mst[:, 1:2], op=ALU.subtract),
             reads=[k5, "mst1"], writes=["mst1"])
        S.op("act", lambda e: e.activation(out=mst[:, 1:2], in_=mst[:, 1:2], func=AF.Sqrt, bias=dcol("eps"), scale=1.0),
             reads=["mst1", "DT"], writes=["mst1"])
        S.op("dve", lambda e: e.reciprocal(out=mst[:, 1:2], in_=mst[:, 1:2]), reads=["mst1"], writes=["mst1"])
        S.op("dve", lambda e: e.scalar_tensor_tensor(out=mst[:, 2:3], in0=mst[:, 0:1], scalar=-1.0, in1=mst[:, 1:2],
                                                      op0=ALU.mult, op1=ALU.mult),
             reads=["mst0", "mst1"], writes=["mst2"])
        S.op("dve", lambda e: e.tensor_scalar(out=mz[:, :, 0:1], in0=mz[:, :, 0:1], scalar1=mst[:, 1:2], scalar2=mst[:, 2:3],
                                               op0=ALU.mult, op1=ALU.add),
             reads=["mz", "mst1", "mst2"], writes=["mz"])
        S.op("dve", lambda e: e.tensor_tensor(out=mz[:, :, 0], in0=mz[:, :, 0], in1=PT[:, g0:g0 + 8], op=ALU.mult),
             reads=["mz", "PT"], writes=["mz"])
        S.op("dve", lambda e: e.tensor_tensor(out=hnext[:, :, 0], in0=mz[:, :, 0], in1=PT[:, b0:b0 + 8], op=ALU.add),
             reads=["mz", "PT"], writes=["hnext"])

    def main_ln_item(ln_idx, t, final):
        a, b = t * TT, (t + 1) * TT
        if ln_idx in (0, 2):
            gname, bname, li = "mix_ln_g", "mix_ln_b", ln_idx // 2
        else:
            gname, bname, li = "ffn_ln_g", "ffn_ln_b", ln_idx // 2
        src = lambda c: hr[:, c, a:b]
        keys = [("hr", c, t) for c in range(NCH)]

        def epi(c):
            if final:
                S.op("dve", lambda e: e.tensor_scalar(out=hr[:, c, a:b], in0=hr[:, c, a:b],
                                                        scalar1=pcol(gname, li * 8 + c), scalar2=pcol(bname, li * 8 + c),
                                                        op0=ALU.mult, op1=ALU.add),
                     reads=[keys[c], "PT"], writes=[keys[c]])
            else:
                S.op("act", lambda e: e.activation(out=hb[:, c, a:b], in_=hr[:, c, a:b], func=AF.Identity,
                                                     bias=pcol(bname, li * 8 + c), scale=pcol(gname, li * 8 + c)),
                     reads=[keys[c], "PT"], writes=[("hb", t)])
                S.op("dve", lambda e: e.tensor_scalar(out=hr[:, c, a:b], in0=hr[:, c, a:b],
                                                        scalar1=dcol("ag", ln_idx * 8 + c), scalar2=dcol("cc", ln_idx * 8 + c),
                                                        op0=ALU.mult, op1=ALU.add),
                     reads=[keys[c], "DT"], writes=[keys[c]])

        def store():
            o0 = max(a, NMETA)
            S.op("sp", lambda e: e.dma_start(out=outT[:, :, o0 - NMETA:b - NMETA], in_=hr[:, :, o0:b]),
                 reads=keys, dma="osem")
        return dict(src=src, keys=keys, nch=NCH, n=TT, ones=ones8, epi=epi, outs=[("hb", t)],
                    after=(store if final else None))

    def out_proj(h, src, first_group, ln_idx, final):
        lo = h * HT
        xk = [("catmem",), ("catmem2",)] if first_group == GA_M0OUT else []
        tiles = [h * 3 + i for i in range(3)]
        for gi in range(4):
            wslot, wk = a_group(first_group + gi)
            for t in tiles:
                a, b = t * TT, (t + 1) * TT
                for ocl in range(2):
                    oc = gi * 2 + ocl
                    bk, kb = bank()
                    S.op("pe", lambda pe, bk=bk, ocl=ocl, a=a, wslot=wslot: mm_generic(pe, bk, TT, wslot, ocl * 128, src, 8, a - lo),
                         reads=[wk] + xk + [("msrc", c, t) for c in range(8)], writes=[kb])
                    S.op("dve", lambda e, bk=bk, oc=oc, a=a, b=b: e.tensor_tensor(out=hr[:, oc, a:b], in0=bk[:, 0:TT],
                                                                                    in1=hr[:, oc, a:b], op=ALU.add),
                         reads=[kb, ("hr", oc, t)], writes=[("hr", oc, t)])
                S.pe_unit()
        if h == 1 and not final:
            mini_ln(ln_idx)
        ln_chain([main_ln_item(ln_idx, t, final) for t in tiles])

    def save_halo():
        S.op("dve", lambda e: e.tensor_copy(out=hsave[:, :, :], in_=hb[:, :, HT - HALO:HT]),
             reads=[("hb", 2)] + [("hbc", c, 0) for c in range(NCH)], writes=["hsave"])

    scale_todo = [(t, c) for t in range(NT) for c in range(NCH)]

    def scale_hr(n):
        for _ in range(n):
            if not scale_todo:
                return
            t, c = scale_todo.pop(0)
            a, b = t * TT, (t + 1) * TT
            S.op("dve", lambda e, c=c, a=a, b=b: e.tensor_scalar(out=hr[:, c, a:b], in0=hr[:, c, a:b], scalar1=ALPHA,
                                                                  scalar2=pcol("b_out_ab", c), op0=ALU.mult, op1=ALU.add),
                 reads=[("hr", c, t), "PT"], writes=[("hr", c, t)])

    def stage_m0(h, ph):
        lo, hi = h * HT, (h + 1) * HT
        elo, ehi = max(0, lo - HALO), min(T, hi + HALO)
        col = lambda tok: tok - lo + HALO
        itiles = tile_ranges(elo, ehi, 3)
        otiles = [(t * TT, (t + 1) * TT, t) for t in range(h * 3, h * 3 + 3)]
        akeys = lambda j: [("abuf", j, x) for x in (0, 1, 2, "pad")]
        ukeys = lambda j: [("ubuf", j, x) for x in (0, 1, 2, "pad")]

        def p_in():
            if h == 0:
                save_halo()
                S.op("pool", lambda e: e.memset(abuf[:, :, 0:HALO], 0.0), writes=[("abuf", j, "pad") for j in range(4)])
                S.op("pool", lambda e: e.memset(ubuf[:, :, 0:HALO], 0.0), writes=[("ubuf", j, "pad") for j in range(4)])
            else:
                S.op("pool", lambda e: e.memset(abuf[:, :, AW - HALO:AW], 0.0), writes=[("abuf", j, "pad") for j in range(4)])
                S.op("pool", lambda e: e.memset(ubuf[:, :, AW - HALO:AW], 0.0), writes=[("ubuf", j, "pad") for j in range(4)])
            for j in range(4):
                wslot, wk = a_group(GA_M0IN + j)
                for ti, (a, b) in enumerate(itiles):
                    n = b - a
                    bg_, kg = bank()
                    bv_, kv = bank()
                    us = (h == 1)
                    S.op("pe", lambda pe, bg_=bg_, a=a, b=b, wslot=wslot: mm_hb(pe, bg_, wslot, 0, a, b, us),
                         reads=[wk] + (["hsave"] if us else []) + hb_keys(a, b, us), writes=[kg])
                    S.op("pe", lambda pe, bv_=bv_, a=a, b=b, wslot=wslot: mm_hb(pe, bv_, wslot, 128, a, b, us),
                         reads=[wk] + (["hsave"] if us else []) + hb_keys(a, b, us), writes=[kv])
                    th, kth = tfslot()
                    vh, kvh = tfslot()
                    S.op("act", lambda e, bg_=bg_, th=th, n=n, j=j: e.activation(out=th[:, 0:n], in_=bg_[:, 0:n], func=AF.Tanh,
                                                                                bias=dcol("hbg", j), scale=0.5),
                         reads=[kg, "DT"], writes=[kth])
                    S.op("act", lambda e, bv_=bv_, vh=vh, n=n, j=j: e.activation(out=vh[:, 0:n], in_=bv_[:, 0:n], func=AF.Identity,
                                                                                bias=dcol("hbv", j), scale=0.5),
                         reads=[kv, "DT"], writes=[kvh])
                    S.op("dve", lambda e, th=th, vh=vh, n=n, j=j, a=a: e.scalar_tensor_tensor(
                        out=abuf[:, j, col(a):col(a) + n], in0=th[:, 0:n], scalar=1.0, in1=vh[:, 0:n], op0=ALU.add, op1=ALU.mult),
                        reads=[kth, kvh], writes=[("abuf", j, ti)])
                    S.pe_unit()
                    if h == 1:
                        scale_hr(3)
            for i in range(2):
                wslot, wk = a_group(GA_M0IN + 4 + i)
                for ti, (a, b) in enumerate(itiles):
                    n = b - a
                    for jl in range(2):
                        j = 2 * i + jl
                        bu, ku = bank()
                        us = (h == 1)
                        S.op("pe", lambda pe, bu=bu, a=a, b=b, wslot=wslot, jl=jl: mm_hb(pe, bu, wslot, jl * 128, a, b, us),
                             reads=[wk] + (["hsave"] if us else []) + hb_keys(a, b, us), writes=[ku])
                        S.op("act", lambda e, bu=bu, n=n, j=j, a=a: e.activation(out=ubuf[:, j, col(a):col(a) + n], in_=bu[:, 0:n],
                                                                                func=AF.Identity, bias=pcol("b_in_ab", 8 + j), scale=1.0),
                             reads=[ku, "PT"], writes=[("ubuf", j, ti)])
                    S.pe_unit()
                    if h == 1:
                        scale_hr(3)
            build31(1)

        def dsel(j):
            return (d31a, [("catmem",)]) if j % 2 == 0 else (d31b, [("d31b",)])

        def build31(j):
            dbuf, dkeys = dsel(j)
            for k in range(31):
                build_diag(dbuf, k, pcol("conv_a_w", k * 4 + j), dkeys)

        def p_conv():
            order = [1, 0, 3, 2]
            u = 0
            pend2 = [None]
            for idx, j in enumerate(order):
                dbuf, dkeys = dsel(j)
                if idx + 1 < 4:
                    build31(order[idx + 1])
                for (a, b, t) in otiles:
                    bc, kc_ = bank()

                    def conv(pe, bc=bc, a=a, j=j, dbuf=dbuf):
                        inst = None
                        for k in range(31):
                            c0 = a - lo + k
                            inst = pe.matmul(bc[:, 0:TT], dbuf[:, k, :], abuf[:, j, c0:c0 + TT], start=(k == 0), stop=(k == 30))
                        return inst
                    S.op("pe", conv, reads=list(dkeys) + akeys(j), writes=[kc_])
                    S.op("act", lambda e, bc=bc, j=j, a=a: e.activation(out=ac32[:, j, a - lo:a - lo + TT], in_=bc[:, 0:TT],
                                                                        func=AF.Identity, bias=pcol("conv_a_b", j), scale=1.0),
                         reads=[kc_, "PT"], writes=[("ac32", j, t)] + RB_KEYS)
                    S.pe_unit()
                    pa, pb_, pt = otiles[u // 4]
                    p2 = pool_unit(pa, pb_, pt, u % 4)
                    if pend2[0] is not None:
                        pend2[0]()
                    pend2[0] = p2
                    u += 1
            pend2[0]()

        def pool_unit(a, b, t, g):
            if True:
                if True:
                    wd = 2 ** (g + 1)
                    bA, kA = bank()

                    def taps(pe, bA=bA, g=g, wd=wd, a=a):
                        inst = None
                        ks = list(range(-(wd // 2), wd // 2))
                        for i, k in enumerate(ks):
                            c0 = col(a) + k
                            lhs = negI[:, g, :] if k == 0 else ident[:, :]
                            inst = pe.matmul(bA[:, 0:TT], lhs, ubuf[:, g, c0:c0 + TT], start=(i == 0), stop=(i == len(ks) - 1))
                        return inst
                    S.op("pe", taps, reads=["ident", "negI"] + ukeys(g), writes=[kA])
                    pb, kp = pbslot()
                    S.op("act", lambda e, bA=bA, pb=pb, wd=wd: e.activation(out=pb[:, 0:TT], in_=bA[:, 0:TT], func=AF.Identity,
                                                                           bias=0.0, scale=1.0 / wd),
                         reads=[kA], writes=[kp])
                    for edge in (0, 1):
                        if (edge == 0 and a != 0) or (edge == 1 and b != T):
                            continue
                        cb_ = g * 32 + 16 * edge
                        e0 = 0 if edge == 0 else TT - 8
                        ei = state["et"]
                        state["et"] = 1 - ei
                        et = etmp[:, ei, :]
                        ke = ("et", ei)
                        S.op("dve", lambda e, bA=bA, e0=e0, cb_=cb_, et=et: e.tensor_tensor(
                            out=et, in0=bA[:, e0:e0 + 8], in1=CT[:, cb_:cb_ + 8], op=ALU.mult),
                            reads=[kA, "CT"], writes=[ke])
                        S.op("dve", lambda e, pb=pb, e0=e0, cb_=cb_, g=g, a=a: e.tensor_tensor(
                            out=pb[:, e0:e0 + 8], in0=ubuf[:, g, col(a) + e0:col(a) + e0 + 8], in1=CT[:, cb_ + 8:cb_ + 16], op=ALU.mult),
                            reads=["CT"] + ukeys(g), writes=[kp])
                        S.op("dve", lambda e, pb=pb, e0=e0, et=et: e.tensor_tensor(
                            out=pb[:, e0:e0 + 8], in0=pb[:, e0:e0 + 8], in1=et, op=ALU.add),
                            reads=[kp, ke], writes=[kp])
                    def part2(pb=pb, kp=kp, g=g, a=a, t=t):
                        bP, kP = bank()
                        S.op("pe", lambda pe: pe.matmul(bP[:, 0:TT], poolw[:, g, :], pb[:, 0:TT], start=True, stop=True),
                             reads=[kp, "poolw"], writes=[kP])
                        S.op("act", lambda e: e.activation(out=cat[:, 4 + g, a - lo:a - lo + TT], in_=bP[:, 0:TT],
                                                             func=AF.Identity, bias=0.0, scale=pcol("pool_scale", g)),
                             reads=[kP, "PT"], writes=[("msrc", 4 + g, t), ("catmem2",)])
                    S.pe_unit()
                    return part2

        def p_lna():
            items = []
            for (a, b, t) in otiles:
                src = lambda c, a=a: ac32[:, c, a - lo:a - lo + TT]
                keys = [("ac32", c, t) for c in range(4)]

                def epi(c, a=a, t=t, src=src, keys=keys):
                    S.op("act", lambda e: e.activation(out=cat[:, c, a - lo:a - lo + TT], in_=src(c), func=AF.Silu,
                                                         bias=pcol("norm_a_b", c), scale=pcol("norm_a_g", c)),
                         reads=[keys[c], "PT"] + RB_KEYS, writes=[("msrc", c, t), ("catmem",)])
                lna = dict(src=src, keys=keys, nch=4, n=TT, ones=ones4, epi=epi, outs=[("msrc", c, t) for c in range(4)],
                           xr=RB_KEYS)
                items.append(lna)
            if h == 0:
                ln_chain(items, base=1, per=7, split0=False)
            else:
                A, B, C = items
                B["alt"] = True
                C["alt"] = 3
                ln_triple(A, B, C)

        def p_out():
            scale_hr(100)
            out_proj(h, cat, GA_M0OUT, 0, final=(n_stages == 1))

        for p in ph:
            {"in": p_in, "conv": p_conv, "lna": p_lna, "out": p_out}[p]()

    def stage_ffn(l, h):
        lo, hi = h * HT, (h + 1) * HT
        if h == 0:
            save_halo()
        tiles = [h * 3 + i for i in range(3)]
        ga0 = GA_F0UP if l == 0 else GA_F1UP
        units = [(j, t) for j in range(NFC) for t in tiles]
        info = {}
        slots = {}

        fb = {"gc": 0, "v": 0}

        def fbank(kind):
            i = fb[kind]
            fb[kind] = (i + 1) % 3
            b = i if kind == "gc" else 3 + i
            return ps[b], ("ps", b)

        def build_fdiag(j):
            for k in range(3):
                build_diag(diag, (j % 2) * 3 + k, pcol("ffn_conv_w", (l * 3 + k) * 22 + j), [("fd", j % 2)])

        def emit_up(j, t):
            if t == tiles[0]:
                slots[j] = a_group(ga0 + j)
            wslot, wk = slots[j]
            a, b = t * TT, (t + 1) * TT
            ea, eb = max(0, a - 1), min(T, b + 1)
            n = eb - ea
            off = ea - (a - 1)
            bG, kG = fbank("gc")
            bV, kV = fbank("v")
            us = (h == 1)
            un = (h == 0 and eb == HT + 1)
            S.op("pe", lambda pe: mm_hb(pe, bG, wslot, 0, ea, eb, us, un),
                 reads=[wk] + (["hsave"] if us else []) + (["hnext"] + hb_keys(ea, HT) if un else hb_keys(ea, eb, us)), writes=[kG])
            S.op("pe", lambda pe: mm_hb(pe, bV, wslot, 128, a, b, False), reads=[wk] + hb_keys(a, b), writes=[kV])
            gb, kgb = gbslot()
            if a == 0:
                S.op("dve", lambda e: e.memset(gb[:, 0:1], 0.0), writes=[kgb])
            if b == T:
                S.op("dve", lambda e: e.memset(gb[:, TT + 1:TT + 2], 0.0), writes=[kgb])
            S.op("act", lambda e: e.activation(out=gb[:, off:off + n], in_=bG[:, 0:n], func=AF.Identity,
                                                 bias=pcol("ffn_b_up", l * 44 + j), scale=1.0),
                 reads=[kG, "PT"], writes=[kgb])
            info[(j, t)] = (bV, kV, gb, kgb)
            S.pe_unit()

        def emit_conv(j, t):
            bV, kV, gb, kgb = info.pop((j, t))
            a = t * TT
            bC, kC = fbank("gc")

            def conv(pe):
                inst = None
                for k in range(3):
                    inst = pe.matmul(bC[:, 0:TT], diag[:, (j % 2) * 3 + k, :], gb[:, k:k + TT], start=(k == 0), stop=(k == 2))
                return inst
            S.op("pe", conv, reads=[kgb, ("fd", j % 2)], writes=[kC])
            sg, ksg = tfslot()
            S.op("act", lambda e: e.activation(out=sg[:, 0:TT], in_=bC[:, 0:TT], func=AF.Silu,
                                                 bias=pcol("ffn_conv_b", l * 22 + j), scale=1.0),
                 reads=[kC, "PT"], writes=[ksg])
            S.op("dve", lambda e: e.scalar_tensor_tensor(out=hid[:, j, a - lo:a - lo + TT], in0=bV[:, 0:TT],
                                                           scalar=pcol("ffn_b_up", l * 44 + 22 + j), in1=sg[:, 0:TT],
                                                           op0=ALU.add, op1=ALU.mult),
                 reads=[kV, ksg, "PT"], writes=[("hid", j, t)])

        if l == 0 and h == 0:
            b_prefetch(RB_SLOTS - 1)
        build_fdiag(0)
        build_fdiag(1)
        prev = None
        for (j, t) in units:
            emit_up(j, t)
            if prev is not None:
                emit_conv(*prev)
                if prev[1] == tiles[-1] and prev[0] + 2 < NFC:
                    build_fdiag(prev[0] + 2)
            prev = (j, t)
        emit_conv(*prev)
        final = (n_stages == 2 * l + 2)
        for oc in range(8):
            wslot, wk = b_slab(l * 8 + oc)
            for t in tiles:
                a, b = t * TT, (t + 1) * TT
                bk, kb = bank()

                def down(pe, bk=bk, a=a, wslot=wslot):
                    inst = None
                    for kc in range(NFC):
                        inst = pe.matmul(bk[:, 0:TT], wslot[:, kc, :], hid[:, kc, a - lo:a - lo + TT],
                                         start=(kc == 0), stop=(kc == NFC - 1))
                    return inst
                S.op("pe", down, reads=[wk] + [("hid", kc, t) for kc in range(NFC)], writes=[kb])
                S.op("dve", lambda e, bk=bk, oc=oc, a=a, b=b: e.tensor_tensor(out=hr[:, oc, a:b], in0=bk[:, 0:TT],
                                                                                in1=hr[:, oc, a:b], op=ALU.add),
                     reads=[kb, ("hr", oc, t)], writes=[("hr", oc, t)])
                S.pe_unit()
        if h == 1 and not final:
            mini_ln(2 * l + 1)
        if final and h == 1:
            A, B, C = [main_ln_item(2 * l + 1, t, final) for t in tiles]
            B["alt"] = 8
            C["alt"] = 9
            ln_triple(A, B, C)
        else:
            ln_chain([main_ln_item(2 * l + 1, t, final) for t in tiles])

    def stage_m1(h):
        lo, hi = h * HT, (h + 1) * HT
        if h == 0:
            save_halo()
            S.op("pool", lambda e: e.memset(cvb[:, :, 0:1], 0.0),
                 writes=[("cv", j, "pad") for j in range(8)] + [("hid", kc, t) for kc in range(NFC) for t in range(NT)])
        else:
            S.op("pool", lambda e: e.memset(cvb[:, :, CVW - 1:CVW], 0.0), writes=[("cv", j, "pad") for j in range(8)])
        tiles = [h * 3 + i for i in range(3)]
        ccol = lambda tok: tok - lo + 1
        for j in range(8):
            wslot, wk = a_group(GA_M1IN + j)
            for i, t in enumerate(tiles):
                a, b = t * TT, (t + 1) * TT
                ea = max(0, a - 1) if i == 0 else a
                eb = min(T, b + 1) if i == 2 else b
                n = eb - ea
                bc, kc_ = bank()
                bv, kv = bank()
                us = (h == 1)
                un = (h == 0 and eb == HT + 1)
                hk = (["hnext"] + hb_keys(ea, HT)) if un else hb_keys(ea, eb, us)
                S.op("pe", lambda pe, bc=bc, ea=ea, eb=eb, wslot=wslot, un=un: mm_hb(pe, bc, wslot, 0, ea, eb, us, un),
                     reads=[wk] + (["hsave"] if us else []) + hk, writes=[kc_])
                S.op("pe", lambda pe, bv=bv, ea=ea, eb=eb, wslot=wslot, un=un: mm_hb(pe, bv, wslot, 128, ea, eb, us, un),
                     reads=[wk] + (["hsave"] if us else []) + hk, writes=[kv])
                vb, kvb = tfslot()
                S.op("act", lambda e, bv=bv, vb=vb, n=n, j=j: e.activation(out=vb[:, 0:n], in_=bv[:, 0:n], func=AF.Identity,
                                                                            bias=pcol("b_in_c", 16 + j), scale=1.0),
                     reads=[kv, "PT"], writes=[kvb])
                S.op("dve", lambda e, bc=bc, vb=vb, n=n, j=j, ea=ea: e.scalar_tensor_tensor(
                    out=cvb[:, j, ccol(ea):ccol(ea) + n], in0=bc[:, 0:n], scalar=pcol("b_in_c", 8 + j), in1=vb[:, 0:n],
                    op0=ALU.add, op1=ALU.mult),
                    reads=[kc_, kvb, "PT"], writes=[("cv", j, i)])
                S.pe_unit()

        def build_cdiag(j):
            for k in range(3):
                build_diag(diag, (j % 2) * 3 + k, pcol("conv_c_w", k * 8 + j), [("fd", j % 2)])
        build_cdiag(0)
        for ii in range(4):
            wslot, wk = a_group(GA_M1IN + 8 + ii)
            for jl in range(2):
                j = 2 * ii + jl
                if j + 1 < 8:
                    build_cdiag(j + 1)
                for t in tiles:
                    a, b = t * TT, (t + 1) * TT
                    bB, kB = bank()
                    S.op("pe", lambda pe, bB=bB, a=a, b=b, wslot=wslot, jl=jl: mm_hb(pe, bB, wslot, jl * 128, a, b, False),
                         reads=[wk] + hb_keys(a, b), writes=[kB])
                    bC, kC = bank()

                    def conv(pe, bC=bC, j=j, a=a):
                        inst = None
                        for k in range(3):
                            c0 = a - lo + k
                            inst = pe.matmul(bC[:, 0:TT], diag[:, (j % 2) * 3 + k, :], cvb[:, j, c0:c0 + TT],
                                             start=(k == 0), stop=(k == 2))
                        return inst
                    S.op("pe", conv, reads=[("fd", j % 2)] + [("cv", j, x) for x in (0, 1, 2, "pad")], writes=[kC])
                    cs, kcs = tfslot()
                    S.op("act", lambda e, bC=bC, cs=cs, j=j: e.activation(out=cs[:, 0:TT], in_=bC[:, 0:TT], func=AF.Identity,
                                                                          bias=pcol("conv_c_b", j), scale=1.0),
                         reads=[kC, "PT"], writes=[kcs])
                    S.op("dve", lambda e, bB=bB, cs=cs, j=j, a=a: e.scalar_tensor_tensor(
                        out=ybuf[:, j, a - lo:a - lo + TT], in0=bB[:, 0:TT], scalar=pcol("b_in_c", j), in1=cs[:, 0:TT],
                        op0=ALU.add, op1=ALU.mult),
                        reads=[kB, kcs, "PT"], writes=[("msrc", j, t)])
                    S.pe_unit()
        out_proj(h, ybuf, GA_M1OUT, 2, final=(n_stages == 3))

    S.op("sp", lambda e: e.dma_start(out=PT[:, :], in_=pt_d[:, :]), writes=["PT"], dma="s_pt")
    S.op("sp", lambda e: e.dma_start(out=CT[:, :], in_=ctab_d[:, 128:256]), writes=["CT"], dma="s_ct")
    for t in range(NT):
        a, b = t * TT, (t + 1) * TT
        pass
    S.op("pool", lambda e: e.dma_start(out=ident[:, :], in_=ctab_d[:, 0:128]), writes=["ident"], dma="s_id")
    XS = HT + HALO
    a_prefetch(0)
    for c in range(NCH):
        S.op("pool", lambda e, c=c: e.dma_start(out=hb[:, c, 0:XS], in_=xT[:, c, 0:XS], max_dma_last_dim=4096),
             writes=[("hbc", c, 0)], dma=("s_xb", c))
    a_prefetch(0)
    for c in range(NCH):
        S.op("pool", lambda e, c=c: e.dma_start(out=hb[:, c, XS:T], in_=xT[:, c, XS:T], max_dma_last_dim=4096),
             writes=[("hbc", c, 1)], dma=("s_xb2", c))
        if c == 3:
            a_prefetch(1)
    S.op("pool", lambda e: e.dma_start(out=poolw[:, :, :], in_=poolw_d[:, :, :]), writes=["poolw"], dma="s_pw")
    for t in range(NT):
        a, b = t * TT, (t + 1) * TT
        S.op("sp", lambda e, a=a, b=b: e.dma_start(out=hr[:, :, a:b], in_=xT[:, :, a:b]),
             reads=[("hbc", NCH - 1, 1)], writes=[("hr", c, t) for c in range(NCH)], dma=("s_x", t))
    a_prefetch(3)
    S.op("dve", lambda e: e.memset(ones8[:, :], 1.0 / 1024.0), writes=["ones"])
    S.op("dve", lambda e: e.memset(ones4[:, :], 1.0 / 512.0), writes=["ones"])
    S.op("dve", lambda e: e.memset(DT[:, DCOL["eps"]:DCOL["eps"] + 1], EPS), writes=["DT"])
    for g in range(4):
        S.op("dve", lambda e, g=g: e.tensor_scalar(out=negI[:, g, :], in0=ident[:, :], scalar1=-float(2 ** (g + 1) - 1),
                                                    scalar2=None, op0=ALU.mult),
             reads=["ident"], writes=["negI"])
    cv_ = PCOL["b_in_ab"]
    S.op("dve", lambda e: e.tensor_scalar(out=DT[:, DCOL["hbg"]:DCOL["hbg"] + 4], in0=PT[:, cv_ + 4:cv_ + 8], scalar1=0.5,
                                           scalar2=None, op0=ALU.mult), reads=["PT"], writes=["DT"])
    S.op("dve", lambda e: e.tensor_scalar(out=DT[:, DCOL["hbv"]:DCOL["hbv"] + 4], in0=PT[:, cv_:cv_ + 4], scalar1=0.5,
                                           scalar2=None, op0=ALU.mult), reads=["PT"], writes=["DT"])
    ln_defs = [("mix_ln_g", "mix_ln_b", 0, "ffn_b_down", 0), ("ffn_ln_g", "ffn_ln_b", 0, "b_out_c", 0),
               ("mix_ln_g", "mix_ln_b", 8, "ffn_b_down", 8)]
    for i, (gn, bn, off, nn, noff) in enumerate(ln_defs):
        g0, b0, n0 = PCOL[gn] + off, PCOL[bn] + off, PCOL[nn] + noff
        S.op("dve", lambda e, i=i, g0=g0: e.tensor_scalar(out=DT[:, DCOL["ag"] + i * 8:DCOL["ag"] + i * 8 + 8], in0=PT[:, g0:g0 + 8],
                                                          scalar1=ALPHA, scalar2=None, op0=ALU.mult), reads=["PT"], writes=["DT"])
        S.op("dve", lambda e, i=i, b0=b0, n0=n0: e.scalar_tensor_tensor(out=DT[:, DCOL["cc"] + i * 8:DCOL["cc"] + i * 8 + 8],
                                                                        in0=PT[:, b0:b0 + 8], scalar=ALPHA, in1=PT[:, n0:n0 + 8],
                                                                        op0=ALU.mult, op1=ALU.add), reads=["PT"], writes=["DT"])
    stage_m0(0, ["in", "conv", "lna"])
    stage_m0(1, ["in"])
    stage_m0(0, ["out"])
    stage_m0(1, ["conv", "lna", "out"])
    if n_stages >= 2:
        stage_ffn(0, 0)
        stage_ffn(0, 1)
    if n_stages >= 3:
        stage_m1(0)
        stage_m1(1)
    if n_stages >= 4:
        stage_ffn(1, 0)
        stage_ffn(1, 1)
    S.flush()
    if max_ops is not None:
        del S.ops[max_ops:]
    n_out = sum(1 for o in S.ops if o.dma == "osem")
    if n_out:
        S.op("sp", lambda e: e.wait_ge(SEMS["osem"], 16 * n_out), reads=[], writes=[])

    S.analyze()
    SEMS = {}
    for k in S.sem_keys():
        SEMS[k] = es.enter_context(nc.semaphore("s_" + "_".join(str(x) for x in (k if isinstance(k, tuple) else (k,)))))
    with nc.Block() as block:
        @block.tensor
        def _(e):
            S.emit("pe", e, SEMS)

        @block.scalar
        def _(e):
            S.emit("act", e, SEMS)

        @block.vector
        def _(e):
            S.emit("dve", e, SEMS)

        @block.gpsimd
        def _(e):
            S.emit("pool", e, SEMS)

        @block.sync
        def _(e):
            S.emit("sp", e, SEMS)
    es.close()
    return nc


_CACHE = {}


def kernel(**inputs):
    n_stages = int(inputs.pop("_n_stages", 4))
    max_ops = inputs.pop("_max_ops", None)
    shared, xts = host_layout(inputs)
    if (n_stages, max_ops) not in _CACHE:
        _CACHE[(n_stages, max_ops)] = build(n_stages, max_ops)
    nc = _CACHE[(n_stages, max_ops)]
    in_maps = [dict(shared, xT=xts[b]) for b in range(8)]
    res = run_bass_kernel_spmd(nc, in_maps, core_ids=list(range(8)))
    outs = []
    for b in range(8):
        o = np.asarray(res.results[b]["outT"], np.float32)
        outs.append(o.transpose(1, 0, 2).reshape(D, SEQ).T)
    return np.ascontiguousarray(np.stack(outs, 0).astype(np.float32))
```

```python
import sys
import numpy as np
from contextlib import ExitStack
import concourse.bass as bass
import concourse.mybir as mybir
from concourse.bass_utils import run_bass_kernel_spmd

F32 = mybir.dt.float32
BF16 = mybir.dt.bfloat16
AF = mybir.ActivationFunctionType
ALU = mybir.AluOpType

D = 1024
NCH = 8
SEQ = 2048
NMETA = 16
T = SEQ + NMETA
TT = 344
NT = 6
HT = 3 * TT
DFF = 2816
NFC = 22
ALPHA = 4.0 ** 0.25
EPS = 1e-5
HALO = 15

PVECS = [
    ("b_in_ab", 12), ("conv_a_w", 31 * 4), ("conv_a_b", 4), ("norm_a_g", 4), ("norm_a_b", 4),
    ("pool_scale", 4), ("b_out_ab", 8), ("b_in_c", 24), ("conv_c_w", 3 * 8), ("conv_c_b", 8),
    ("b_out_c", 8), ("mix_ln_g", 16), ("mix_ln_b", 16), ("ffn_b_up", 88), ("ffn_conv_w", 2 * 3 * 22),
    ("ffn_conv_b", 44), ("ffn_b_down", 16), ("ffn_ln_g", 16), ("ffn_ln_b", 16),
]
PCOL = {}
_c = 0
for _n, _k in PVECS:
    PCOL[_n] = _c
    _c += _k
NPCOL = _c

DCOL = {"hbg": 0, "hbv": 4, "ag": 8, "cc": 8 + 32, "eps": 8 + 64}
NDCOL = 8 + 64 + 1

GA_M0IN, GA_M0OUT, GA_F0UP, GA_M1IN, GA_M1OUT, GA_F1UP = 0, 6, 10, 32, 44, 48
NGA = 70


def _vec_cols(v):
    v = np.asarray(v, np.float32).reshape(-1, 128)
    return np.ascontiguousarray(v.T)


def _group_cols(W, cols):
    sub = W[:, cols]
    return np.ascontiguousarray(sub.reshape(8, 128, -1).transpose(1, 0, 2))


def _rng(a, n=128):
    return list(range(a, a + n))


def host_layout(inp):
    f = lambda k: np.asarray(inp[k], np.float32)
    pt = np.concatenate([
        _vec_cols(f("b_in_ab")[0]), _vec_cols(f("conv_a_w")[0]), _vec_cols(f("conv_a_b")[0]),
        _vec_cols(f("norm_a_g")[0]), _vec_cols(f("norm_a_b")[0]), _vec_cols(f("pool_scale")[0]),
        _vec_cols(f("b_out_ab")[0]), _vec_cols(f("b_in_c")[0]), _vec_cols(f("conv_c_w")[0]),
        _vec_cols(f("conv_c_b")[0]), _vec_cols(f("b_out_c")[0]), _vec_cols(f("mix_ln_g")),
        _vec_cols(f("mix_ln_b")), _vec_cols(f("ffn_b_up")), _vec_cols(f("ffn_conv_w")),
        _vec_cols(f("ffn_conv_b")), _vec_cols(f("ffn_b_down")), _vec_cols(f("ffn_ln_g")),
        _vec_cols(f("ffn_ln_b")),
    ], axis=1)
    assert pt.shape == (128, NPCOL)
    groups = []
    w = f("w_in_ab")[0]
    for j in range(4):
        groups.append(_group_cols(w, _rng(512 + j * 128) + _rng(j * 128)))
    for i in range(2):
        groups.append(_group_cols(w, _rng(1024 + i * 256, 256)))
    w = f("w_out_ab")[0]
    for i in range(4):
        groups.append(_group_cols(w, _rng(i * 256, 256)))
    w = f("ffn_w_up")[0]
    for j in range(22):
        groups.append(_group_cols(w, _rng(j * 128) + _rng(DFF + j * 128)))
    w = f("w_in_c")[0]
    for j in range(8):
        groups.append(_group_cols(w, _rng(1024 + j * 128) + _rng(2048 + j * 128)))
    for i in range(4):
        groups.append(_group_cols(w, _rng(i * 256, 256)))
    w = f("w_out_c")[0]
    for i in range(4):
        groups.append(_group_cols(w, _rng(i * 256, 256)))
    w = f("ffn_w_up")[1]
    for j in range(22):
        groups.append(_group_cols(w, _rng(j * 128) + _rng(DFF + j * 128)))
    wA = np.stack(groups, 0)
    assert wA.shape[0] == NGA
    slabs = []
    for l in range(2):
        w = f("ffn_w_down")[l]
        for oc in range(8):
            sub = w[:, oc * 128:(oc + 1) * 128]
            slabs.append(np.ascontiguousarray(sub.reshape(22, 128, 128).transpose(1, 0, 2)))
    wB = np.stack(slabs, 0)
    poolw = np.ascontiguousarray(f("pool_w")[0].transpose(1, 0, 2))
    ctab = np.zeros((128, 256), np.float32)
    ctab[:, :128] = np.eye(128, dtype=np.float32)
    for g in range(4):
        wd = 2 ** (g + 1)
        base = 128 + g * 32
        for t in range(8):
            cnt = min(t + wd // 2, wd) if t < wd // 2 else wd
            cnt = t + wd // 2 if t < wd // 2 else wd
            ctab[:, base + t] = 1.0 / cnt
            ctab[:, base + 8 + t] = (wd - cnt) / cnt
        for i in range(8):
            t = T - 8 + i
            cnt = (T - t + wd // 2) if t > T - wd // 2 else wd
            ctab[:, base + 16 + i] = 1.0 / cnt
            ctab[:, base + 24 + i] = (wd - cnt) / cnt
    x = f("x")
    meta = f("meta_tokens")
    xts = []
    for b in range(x.shape[0]):
        h0 = np.concatenate([meta, x[b]], axis=0)
        xts.append(np.ascontiguousarray(h0.T.reshape(8, 128, T).transpose(1, 0, 2)))
    return dict(pt=pt, wA=wA, wB=wB, poolw=poolw, ctab=ctab), xts


class Op:
    __slots__ = ("eng", "fn", "reads", "writes", "dma", "deps", "signal", "sigval", "waits", "tag", "pos")

    def __init__(self, eng, fn, reads, writes, dma):
        self.eng, self.fn, self.reads, self.writes, self.dma = eng, fn, tuple(reads), tuple(writes), dma
        self.deps = []
        self.signal = False
        self.sigval = 0
        self.waits = []


class Sched:
    ENGS = ("pe", "act", "dve", "pool", "sp")

    def __init__(self):
        self.ops = []
        self.pending = []
        self.in_deferred = False

    def op(self, eng, fn, reads=(), writes=(), dma=None):
        touched = set(reads) | set(writes)
        if self.pending and not self.in_deferred:
            last = -1
            for i, it in enumerate(self.pending):
                if it[1] & touched:
                    last = i
            if last >= 0:
                items = self.pending[:last + 1]
                del self.pending[:last + 1]
                self._run(items)
        o = Op(eng, fn, reads, writes, dma)
        o.tag = sys._getframe(1).f_lineno
        self.ops.append(o)
        return o

    def _run(self, items):
        assert not self.in_deferred
        self.in_deferred = True
        try:
            for it in items:
                it[2]()
        finally:
            self.in_deferred = False

    def defer(self, countdown, writes, fn):
        assert not self.in_deferred
        self.pending.append([countdown, set(writes), fn])

    def pe_unit(self):
        if self.in_deferred or not self.pending:
            return
        for it in self.pending:
            it[0] -= 1
        due = [i for i, it in enumerate(self.pending) if it[0] <= 0]
        if not due:
            return
        last = max(due)
        items = self.pending[:last + 1]
        del self.pending[:last + 1]
        self._run(items)

    def flush(self):
        items = self.pending[:]
        del self.pending[:]
        self._run(items)

    def analyze(self):
        last_w = {}
        readers = {}
        for i, o in enumerate(self.ops):
            o.pos = i
        for o in self.ops:
            deps = []
            o.writes = o.writes + tuple(r for r in o.reads if isinstance(r, tuple) and r[0] == "ps" and r not in o.writes)
            for r in o.reads:
                w = last_w.get(r)
                if w is not None:
                    deps.append(w)
            for r in o.writes:
                w = last_w.get(r)
                if w is not None:
                    deps.append(w)
                deps.extend(readers.get(r, ()))
            seen = set()
            latest = {}
            for d in deps:
                if d is o or id(d) in seen:
                    continue
                seen.add(id(d))
                if d.dma is None and d.eng == "pe" and o.eng == "pe":
                    continue
                if d.dma is not None:
                    o.deps.append(d)
                else:
                    cur = latest.get(d.eng)
                    if cur is None or d.pos > cur.pos:
                        latest[d.eng] = d
            for d in latest.values():
                o.deps.append(d)
                d.signal = True
            for r in o.reads:
                if r not in o.writes:
                    readers.setdefault(r, []).append(o)
            for r in o.writes:
                last_w[r] = o
                readers[r] = []
        cnt = {}
        for o in self.ops:
            if o.dma is not None:
                cnt[o.dma] = cnt.get(o.dma, 0) + 16
                o.sigval = cnt[o.dma]
            elif o.signal:
                cnt[o.eng] = cnt.get(o.eng, 0) + 1
                o.sigval = cnt[o.eng]
        waited = {e: {} for e in self.ENGS}
        for o in self.ops:
            need = {}
            for d in o.deps:
                key = d.dma if d.dma is not None else d.eng
                if d.sigval > need.get(key, 0):
                    need[key] = d.sigval
            wd = waited[o.eng]
            for key, v in need.items():
                if v > wd.get(key, 0):
                    wd[key] = v
                    o.waits.append((key, v))
        return cnt

    def sem_keys(self):
        keys = []
        for o in self.ops:
            k = o.dma if o.dma is not None else (o.eng if o.signal else None)
            if k is not None and k not in keys:
                keys.append(k)
        return keys

    def emit(self, eng, handle, sems):
        for o in self.ops:
            if o.eng != eng:
                continue
            for key, v in o.waits:
                handle.wait_ge(sems[key], v)
            inst = o.fn(handle)
            if o.dma is not None:
                inst.then_inc(sems[o.dma], 16)
            elif o.signal:
                inst.then_inc(sems[eng], 1)


def tile_ranges(lo, hi, n):
    step = (hi - lo + n - 1) // n
    out = []
    a = lo
    while a < hi:
        b = min(hi, a + step)
        out.append((a, b))
        a = b
    return out


def build(n_stages=4, max_ops=None):
    nc = bass.Bass("TRN2", target_bir_lowering=False)
    S = Sched()
    es = ExitStack()

    def dram(name, shape, kind):
        return nc.dram_tensor(name, shape, F32, kind=kind).ap()

    xT = dram("xT", [128, NCH, T], "ExternalInput")
    pt_d = dram("pt", [128, NPCOL], "ExternalInput")
    wA = dram("wA", [NGA, 128, 8, 256], "ExternalInput")
    wB = dram("wB", [16, 128, NFC, 128], "ExternalInput")
    poolw_d = dram("poolw", [128, 4, 128], "ExternalInput")
    ctab_d = dram("ctab", [128, 256], "ExternalInput")
    outT = dram("outT", [128, NCH, SEQ], "ExternalOutput")

    def sb(name, shape, dt):
        return es.enter_context(nc.sbuf_tensor(name, shape, dt))

    hb = sb("hb", [128, NCH, T], BF16)
    hr = sb("hr", [128, NCH, T], F32)
    ARENA_E = NFC * HT
    arena = sb("arena", [128, ARENA_E], BF16)
    ringA = sb("ringA", [128, 4, 8, 256], BF16)
    RB_SLOTS = 3
    ringB = sb("ringB", [128, RB_SLOTS, NFC, 128], BF16)
    PT = sb("PT", [128, NPCOL], F32)
    DT = sb("DT", [128, NDCOL], F32)
    CT = sb("CT", [128, 128], F32)
    ident = sb("ident", [128, 128], BF16)
    negI = sb("negI", [128, 4, 128], BF16)
    ones8 = sb("ones8", [128, 128], BF16)
    ones4 = sb("ones4", [128, 128], BF16)
    poolw = sb("poolw_sb", [128, 4, 128], BF16)
    diag = sb("diag", [128, 6, 128], BF16)
    hsave = sb("hsave", [128, NCH, HALO], BF16)
    zz = sb("zz", [128, 2, NCH, TT], BF16)
    zb = zz[:, 0]
    zsq = zz[:, 1]
    ZZ_KEYS = [("zb", c) for c in range(NCH)] + [("zsq", c) for c in range(NCH)]
    st = sb("st", [128, 1, 3, TT], F32)
    NTF = 4
    tf = sb("tf", [128, NTF, 352], F32)
    NGB = 3
    gbuf = sb("gbuf", [128, NGB, 352], BF16)
    pbuf = sb("pbuf", [128, 2, TT], BF16)
    etmp = sb("etmp", [128, 2, 8], F32)
    mz = sb("mz", [128, NCH, 2], F32)
    mzb = sb("mzb", [128, NCH, 2], BF16)
    mzsq = sb("mzsq", [128, NCH, 2], BF16)
    mst = sb("mst", [128, 4], F32)
    hnext = sb("hnext", [128, NCH, 2], BF16)
    ps = [es.enter_context(nc.psum_tensor(f"ps{i}", [128, 512], F32)) for i in range(8)]

    def aview(off, c, w):
        return arena[:, off:off + c * w].rearrange("p (c w) -> p c w", c=c)

    AW = HT + 2 * HALO
    abuf = aview(0, 4, AW)
    ubuf = aview(4 * AW, 4, AW)
    cat = aview(8 * AW, 8, HT)
    d31b = arena[:, 8 * AW + 8 * HT: 8 * AW + 8 * HT + 31 * 128].rearrange("p (k m) -> p k m", k=31)
    d31a = arena[:, 8 * AW: 8 * AW + 31 * 128].rearrange("p (k m) -> p k m", k=31)
    assert 8 * AW + 8 * HT + 31 * 128 <= ARENA_E
    CVW = HT + 2
    cvb = aview(0, 8, CVW)
    ybuf = aview(8 * CVW, 8, HT)
    hid = aview(0, NFC, HT)
    ac32 = ringB[:, :, :, :].rearrange("p s k m -> p (s k m)")[:, 0:2 * 4 * HT].bitcast(F32) \
        .rearrange("p (c w) -> p c w", c=4)
    assert 2 * 4 * HT <= RB_SLOTS * NFC * 128
    RB_KEYS = [("rB", s) for s in range(RB_SLOTS)]
    zz2 = arena[:, 0:2 * NCH * TT].rearrange("p (a c w) -> p a c w", a=2, c=NCH)
    zz3 = arena[:, 8 * AW + 8 * HT: 8 * AW + 8 * HT + 2 * 4 * TT].rearrange("p (a c w) -> p a c w", a=2, c=4)
    HID_H1_KEYS = [("hid", kc, t) for kc in range(NFC) for t in (3, 4, 5)]

    def pcol(name, idx):
        c = PCOL[name] + idx
        return PT[:, c:c + 1]

    def dcol(name, idx=0):
        c = DCOL[name] + idx
        return DT[:, c:c + 1]

    state = {"bank": 0, "tf": 0, "gb": 0, "pb": 0, "st": 0, "et": 0}

    NGBANK = 6

    def bank():
        b = state["bank"]
        state["bank"] = (b + 1) % NGBANK
        return ps[b], ("ps", b)

    def tfslot():
        i = state["tf"]
        state["tf"] = (i + 1) % NTF
        return tf[:, i, :], ("tf", i)

    def gbslot():
        i = state["gb"]
        state["gb"] = (i + 1) % NGB
        return gbuf[:, i, :], ("gb", i)

    def pbslot():
        i = state["pb"]
        state["pb"] = (i + 1) % 2
        return pbuf[:, i, :], ("pb", i)

    ring = {"a_next": 0, "b_next": 0}
    a_order = []
    b_order = []

    def plan_stage_groups():
        for h in range(2):
            a_order.extend(range(GA_M0IN, GA_M0IN + 6))
        for h in range(2):
            a_order.extend(range(GA_M0OUT, GA_M0OUT + 4))
        for h in range(2):
            a_order.extend(range(GA_F0UP, GA_F0UP + 22))
            b_order.extend(range(0, 8))
        for h in range(2):
            a_order.extend(range(GA_M1IN, GA_M1IN + 12))
            a_order.extend(range(GA_M1OUT, GA_M1OUT + 4))
        for h in range(2):
            a_order.extend(range(GA_F1UP, GA_F1UP + 22))
            b_order.extend(range(8, 16))

    plan_stage_groups()
    a_use = {"i": 0}
    b_use = {"i": 0}

    def a_prefetch(upto):
        while ring["a_next"] <= min(upto, len(a_order) - 1):
            n = ring["a_next"]
            s = n % 4
            g = a_order[n]
            S.op("pool", lambda e, s=s, g=g: e.dma_start(out=ringA[:, s], in_=wA[g], max_dma_last_dim=4096),
                 writes=[("rA", s)], dma=("dA", s))
            ring["a_next"] += 1

    def a_group(expect):
        n = a_use["i"]
        assert a_order[n] == expect, (n, a_order[n], expect)
        a_use["i"] += 1
        a_prefetch(n + 3)
        s = n % 4
        return ringA[:, s], ("rA", s)

    def b_prefetch(upto):
        while ring["b_next"] <= min(upto, len(b_order) - 1):
            n = ring["b_next"]
            s = n % RB_SLOTS
            g = b_order[n]
            S.op("pool", lambda e, s=s, g=g: e.dma_start(out=ringB[:, s], in_=wB[g], max_dma_last_dim=4096),
                 writes=[("rB", s)], dma=("dB", s))
            ring["b_next"] += 1

    def b_slab(expect):
        n = b_use["i"]
        assert b_order[n] == expect
        b_use["i"] += 1
        b_prefetch(n + RB_SLOTS - 1)
        s = n % RB_SLOTS
        return ringB[:, s], ("rB", s)

    def tile_of(tok):
        return min(tok // TT, NT - 1)

    def hb_keys(a, b, use_save=False):
        if use_save and a < HT:
            a = HT
        ks = [("hb", t) for t in range(tile_of(a), tile_of(b - 1) + 1)]
        if a < HT + HALO:
            ks += [("hbc", c, 0) for c in range(NCH)]
        if b > HT + HALO:
            ks += [("hbc", c, 1) for c in range(NCH)]
        return ks

    def mm_hb(pe, bk, wslot, c0, a, b, use_save, use_next=False):
        parts = []
        if use_next:
            assert b == HT + 1 and a < HT
            parts.append((hb, a, HT, 0, HT - a))
            parts.append((hnext, 0, 1, HT - a, HT - a + 1))
        elif use_save and a < HT:
            parts.append((hsave, a - (HT - HALO), HT - (HT - HALO), 0, HT - a))
            parts.append((hb, HT, b, HT - a, b - a))
        else:
            parts.append((hb, a, b, 0, b - a))
        inst = None
        for (src, sa, sbnd, o0, o1) in parts:
            for kc in range(8):
                inst = pe.matmul(bk[:, o0:o1], wslot[:, kc, c0:c0 + 128], src[:, kc, sa:sbnd],
                                 start=(kc == 0), stop=(kc == 7))
        return inst

    def mm_generic(pe, bk, n, wslot, c0, src, nk, sa):
        inst = None
        for kc in range(nk):
            inst = pe.matmul(bk[:, 0:n], wslot[:, kc, c0:c0 + 128], src[:, kc, sa:sa + n],
                             start=(kc == 0), stop=(kc == nk - 1))
        return inst

    def build_diag(dst, k, colap, keys):
        S.op("dve", lambda e: e.tensor_scalar(out=dst[:, k, :], in0=ident[:, :], scalar1=colap, scalar2=None, op0=ALU.mult),
             reads=["ident", "PT"], writes=list(keys))

    def ln_prelude(it):
        src, keys, nch, n = it["src"], it["keys"], it["nch"], it["n"]
        zo = 4 if it.get("alt") else 0
        if it.get("alt") == 3:
            for c in range(nch):
                S.op("act", lambda e, c=c: e.activation(out=zz3[:, 1, c, 0:n], in_=src(c), func=AF.Square),
                     reads=[keys[c]] + it.get("xr", []), writes=[("zz3q", c), ("d31b",)])
                S.op("dve", lambda e, c=c: e.tensor_copy(out=zz3[:, 0, c, 0:n], in_=src(c)),
                     reads=[keys[c]] + it.get("xr", []), writes=[("zz3b", c), ("d31b",)])
            return
        if it.get("alt") == 8:
            for c in range(nch):
                S.op("act", lambda e, c=c: e.activation(out=zz2[:, 1, c, 0:n], in_=src(c), func=AF.Square),
                     reads=[keys[c]], writes=[("zz2q", c)] + HID_H1_KEYS)
                S.op("dve", lambda e, c=c: e.tensor_copy(out=zz2[:, 0, c, 0:n], in_=src(c)),
                     reads=[keys[c]], writes=[("zz2b", c)] + HID_H1_KEYS)
            return
        for f in ln_prelude_ops(it):
            f()

    def ln_prelude_ops(it):
        src, keys, nch, n = it["src"], it["keys"], it["nch"], it["n"]
        zo = 4 if it.get("alt") else 0
        assert it.get("alt") != 8

        def one(c):
            S.op("act", lambda e: e.activation(out=zsq[:, zo + c, 0:n], in_=src(c), func=AF.Square),
                 reads=[keys[c]] + it.get("xr", []), writes=[("zsq", zo + c)])
            S.op("dve", lambda e: e.tensor_copy(out=zb[:, zo + c, 0:n], in_=src(c)),
                 reads=[keys[c]] + it.get("xr", []), writes=[("zb", zo + c)])
        return [(lambda c=c: one(c)) for c in range(nch)]

    def ln_pieces(it, split=False):
        src, keys, nch, n, ones, epilogue = it["src"], it["keys"], it["nch"], it["n"], it["ones"], it["epi"]
        xr = it.get("xr", [])
        zbv, zqv, zbk, zqk = zb, zsq, "zb", "zsq"
        if it.get("alt"):
            if it.get("alt") == 8:
                zo = 0
                zbv, zqv, zbk, zqk = zz2[:, 0], zz2[:, 1], "zz2b", "zz2q"
            else:
                assert nch <= 4
                zo = 4
            mean_sb, st_a = tf[:, 0, 0:n], tf[:, 1, 0:n]
            km_, ka_ = ("tf", 0), ("tf", 1)
            bm, km = ps[4], ("ps", 4)
            be, ke = ps[5], ("ps", 5)
        else:
            zo = 0
            mean_sb, st_a = st[:, 0, 0, 0:n], st[:, 0, 1, 0:n]
            km_, ka_ = ("st", "m"), ("st", "a")
            bm, km = ps[6], ("ps", 6)
            be, ke = ps[7], ("ps", 7)

        if it.get("alt") == 3:
            zo = 0
            zbv, zqv, zbk, zqk = zz3[:, 0], zz3[:, 1], "zz3b", "zz3q"
            mean_sb, st_a = tf[:, 2, 0:n], tf[:, 3, 0:n]
            km_, ka_ = ("tf", 2), ("tf", 3)
            bm, km = ps[2], ("ps", 2)
            be, ke = ps[3], ("ps", 3)

        def p0a():
            def stats(pe):
                inst = None
                for c in range(nch):
                    inst = pe.matmul(bm[:, 0:n], ones[:, :], zbv[:, zo + c, 0:n], start=(c == 0), stop=(c == nch - 1))
                for c in range(nch):
                    inst = pe.matmul(be[:, 0:n], ones[:, :], zqv[:, zo + c, 0:n], start=(c == 0), stop=(c == nch - 1))
                return inst
            S.op("pe", stats, reads=[(zbk, zo + c) for c in range(nch)] + [(zqk, zo + c) for c in range(nch)] + ["ones"],
                 writes=[km, ke])
            S.op("act", lambda e: e.activation(out=mean_sb, in_=bm[:, 0:n], func=AF.Copy), reads=[km], writes=[km_])
            S.op("act", lambda e: e.activation(out=st_a, in_=bm[:, 0:n], func=AF.Square), reads=[km], writes=[ka_])
            S.op("dve", lambda e: e.tensor_tensor(out=st_a, in0=be[:, 0:n], in1=st_a, op=ALU.subtract),
                 reads=[ke, ka_], writes=[ka_])

        def p0b():
            S.op("act", lambda e: e.activation(out=st_a, in_=st_a, func=AF.Sqrt, bias=dcol("eps"), scale=1.0),
                 reads=[ka_, "DT"], writes=[ka_])

        def p0c():
            S.op("dve", lambda e: e.reciprocal(out=bm[:, 0:n], in_=st_a), reads=[ka_], writes=[km])
            S.op("dve", lambda e: e.scalar_tensor_tensor(out=be[:, 0:n], in0=mean_sb, scalar=-1.0, in1=bm[:, 0:n],
                                                          op0=ALU.mult, op1=ALU.mult),
                 reads=[km_, km], writes=[ke])

        def piece0():
            p0a()
            p0b()
            p0c()

        def stA(c):
            S.op("dve", lambda e: e.tensor_tensor(out=src(c), in0=src(c), in1=bm[:, 0:n], op=ALU.mult),
                 reads=[keys[c], km] + xr, writes=[keys[c]])

        def stB(c):
            S.op("dve", lambda e: e.tensor_tensor(out=src(c), in0=src(c), in1=be[:, 0:n], op=ALU.add),
                 reads=[keys[c], ke] + xr, writes=[keys[c]])

        def stC(c):
            epilogue(c)
            if c == nch - 1 and it.get("after"):
                it["after"]()

        def chunk(c):
            stA(c)
            stB(c)
            stC(c)
        if split:
            piece0.parts = [p0a, p0b, p0c]
            return piece0, [(lambda c=c: stA(c)) for c in range(nch)], [(lambda c=c: stB(c)) for c in range(nch)], \
                [(lambda c=c: stC(c)) for c in range(nch)]
        return [piece0] + [(lambda c=c: chunk(c)) for c in range(nch)]

    def ln_part2(it):
        for p in ln_pieces(it):
            p()

    def ln_chain(items, base=1, per=13, split0=True):
        ln_prelude(items[0])
        for i in range(len(items)):
            it = items[i]
            nch = it["nch"]
            touched = set(ZZ_KEYS) | set(it["keys"]) | set(it["outs"])
            if i + 1 < len(items):
                touched |= set(items[i + 1]["keys"])
            p0, A, B, C = ln_pieces(it, split=True)
            steps = [[f] for f in p0.parts] if split0 else [[p0]]
            n0 = len(steps)
            for k in range(nch + 2):
                ops = []
                if k < nch:
                    ops.append(A[k])
                if 0 <= k - 1 < nch:
                    ops.append(B[k - 1])
                if 0 <= k - 2 < nch:
                    ops.append(C[k - 2])
                steps.append(ops)
            if i + 1 < len(items):
                for jn, f in enumerate(ln_prelude_ops(items[i + 1])):
                    steps[n0 + jn // 2].append(f)
            assert len(steps) <= per
            for k, ops in enumerate(steps):
                def run(ops=ops):
                    for f in ops:
                        f()
                S.defer(base + per * i + k, touched, run)

    def ln_pair(A, B, C):
        def skew(sets, n):
            for k in range(n + 2):
                for (_, sa, sb_, sc) in sets:
                    if k < n:
                        sa[k]()
                for (_, sa, sb_, sc) in sets:
                    if 0 <= k - 1 < n:
                        sb_[k - 1]()
                for (_, sa, sb_, sc) in sets:
                    if 0 <= k - 2 < n:
                        sc[k - 2]()
        ln_prelude(A)
        ln_prelude(B)
        pa = ln_pieces(A, split=True)
        pb_ = ln_pieces(B, split=True)
        pa[0]()
        pb_[0]()
        ln_prelude(C)
        skew([pa, pb_], A["nch"])
        pc = ln_pieces(C, split=True)
        pc[0]()
        skew([pc], C["nch"])

    def ln_triple(A, B, C):
        its = (A, B, C)
        for it in its:
            ln_prelude(it)
        sets = [ln_pieces(it, split=True) for it in its]
        for parts in zip(*[p[0].parts for p in sets]):
            for f in parts:
                f()
        n = A["nch"]
        for k in range(n + 2):
            for (_, sa, sb_, sc) in sets:
                if k < n:
                    sa[k]()
            for (_, sa, sb_, sc) in sets:
                if 0 <= k - 1 < n:
                    sb_[k - 1]()
            for (_, sa, sb_, sc) in sets:
                if 0 <= k - 2 < n:
                    sc[k - 2]()

    def mini_ln(ln_idx):
        if ln_idx in (0, 2):
            gname, bname, li = "mix_ln_g", "mix_ln_b", ln_idx // 2
        else:
            gname, bname, li = "ffn_ln_g", "ffn_ln_b", ln_idx // 2
        g0, b0 = PCOL[gname] + li * 8, PCOL[bname] + li * 8
        b4, k4 = ps[4], ("ps", 4)
        b5, k5 = ps[5], ("ps", 5)
        S.op("dve", lambda e: e.tensor_copy(out=mz[:, :, 0:1], in_=hr[:, :, HT:HT + 1]),
             reads=[("hr", c, 3) for c in range(NCH)], writes=["mz"])
        S.op("act", lambda e: e.activation(out=mzsq[:, :, 0:1], in_=mz[:, :, 0:1], func=AF.Square), reads=["mz"], writes=["mzsq"])
        S.op("dve", lambda e: e.tensor_copy(out=mzb[:, :, 0:1], in_=mz[:, :, 0:1]), reads=["mz"], writes=["mzb"])

        def stats(pe):
            inst = None
            for c in range(NCH):
                inst = pe.matmul(b4[:, 0:1], ones8[:, :], mzb[:, c, 0:1], start=(c == 0), stop=(c == NCH - 1))
            for c in range(NCH):
                inst = pe.matmul(b5[:, 0:1], ones8[:, :], mzsq[:, c, 0:1], start=(c == 0), stop=(c == NCH - 1))
            return inst
        S.op("pe", stats, reads=["mzb", "mzsq", "ones"], writes=[k4, k5])
        S.op("act", lambda e: e.activation(out=mst[:, 0:1], in_=b4[:, 0:1], func=AF.Copy), reads=[k4], writes=["mst0"])
        S.op("act", lambda e: e.activation(out=mst[:, 1:2], in_=b4[:, 0:1], func=AF.Square), reads=[k4], writes=["mst1"])
        S.op("dve", lambda e: e.tensor_tensor(out=mst[:, 1:2], in0=b5[:, 0:1], in1=mst[:, 1:2], op=ALU.subtract),
             reads=[k5, "mst1"], writes=["mst1"])
        S.op("act", lambda e: e.activation(out=mst[:, 1:2], in_=mst[:, 1:2], func=AF.Sqrt, bias=dcol("eps"), scale=1.0),
             reads=["mst1", "DT"], writes=["mst1"])
        S.op("dve", lambda e: e.reciprocal(out=mst[:, 1:2], in_=mst[:, 1:2]), reads=["mst1"], writes=["mst1"])
        S.op("dve", lambda e: e.scalar_tensor_tensor(out=mst[:, 2:3], in0=mst[:, 0:1], scalar=-1.0, in1=mst[:, 1:2],
                                                      op0=ALU.mult, op1=ALU.mult),
             reads=["mst0", "mst1"], writes=["mst2"])
        S.op("dve", lambda e: e.tensor_scalar(out=mz[:, :, 0:1], in0=mz[:, :, 0:1], scalar1=mst[:, 1:2], scalar2=mst[:, 2:3],
                                               op0=ALU.mult, op1=ALU.add),
             reads=["mz", "mst1", "mst2"], writes=["mz"])
        S.op("dve", lambda e: e.tensor_tensor(out=mz[:, :, 0], in0=mz[:, :, 0], in1=PT[:, g0:g0 + 8], op=ALU.mult),
             reads=["mz", "PT"], writes=["mz"])
        S.op("dve", lambda e: e.tensor_tensor(out=hnext[:, :, 0], in0=mz[:, :, 0], in1=PT[:, b0:b0 + 8], op=ALU.add),
             reads=["mz", "PT"], writes=["hnext"])

    def main_ln_item(ln_idx, t, final):
        a, b = t * TT, (t + 1) * TT
        if ln_idx in (0, 2):
            gname, bname, li = "mix_ln_g", "mix_ln_b", ln_idx // 2
        else:
            gname, bname, li = "ffn_ln_g", "ffn_ln_b", ln_idx // 2
        src = lambda c: hr[:, c, a:b]
        keys = [("hr", c, t) for c in range(NCH)]

        def epi(c):
            if final:
                S.op("dve", lambda e: e.tensor_scalar(out=hr[:, c, a:b], in0=hr[:, c, a:b],
                                                        scalar1=pcol(gname, li * 8 + c), scalar2=pcol(bname, li * 8 + c),
                                                        op0=ALU.mult, op1=ALU.add),
                     reads=[keys[c], "PT"], writes=[keys[c]])
            else:
                S.op("act", lambda e: e.activation(out=hb[:, c, a:b], in_=hr[:, c, a:b], func=AF.Identity,
                                                     bias=pcol(bname, li * 8 + c), scale=pcol(gname, li * 8 + c)),
                     reads=[keys[c], "PT"], writes=[("hb", t)])
                S.op("dve", lambda e: e.tensor_scalar(out=hr[:, c, a:b], in0=hr[:, c, a:b],
                                                        scalar1=dcol("ag", ln_idx * 8 + c), scalar2=dcol("cc", ln_idx * 8 + c),
                                                        op0=ALU.mult, op1=ALU.add),
                     reads=[keys[c], "DT"], writes=[keys[c]])

        def store():
            o0 = max(a, NMETA)
            S.op("sp", lambda e: e.dma_start(out=outT[:, :, o0 - NMETA:b - NMETA], in_=hr[:, :, o0:b]),
                 reads=keys, dma="osem")
        return dict(src=src, keys=keys, nch=NCH, n=TT, ones=ones8, epi=epi, outs=[("hb", t)],
                    after=(store if final else None))

    def out_proj(h, src, first_group, ln_idx, final):
        lo = h * HT
        xk = [("catmem",), ("catmem2",)] if first_group == GA_M0OUT else []
        tiles = [h * 3 + i for i in range(3)]
        for gi in range(4):
            wslot, wk = a_group(first_group + gi)
            for t in tiles:
                a, b = t * TT, (t + 1) * TT
                for ocl in range(2):
                    oc = gi * 2 + ocl
                    bk, kb = bank()
                    S.op("pe", lambda pe, bk=bk, ocl=ocl, a=a, wslot=wslot: mm_generic(pe, bk, TT, wslot, ocl * 128, src, 8, a - lo),
                         reads=[wk] + xk + [("msrc", c, t) for c in range(8)], writes=[kb])
                    S.op("dve", lambda e, bk=bk, oc=oc, a=a, b=b: e.tensor_tensor(out=hr[:, oc, a:b], in0=bk[:, 0:TT],
                                                                                    in1=hr[:, oc, a:b], op=ALU.add),
                         reads=[kb, ("hr", oc, t)], writes=[("hr", oc, t)])
                S.pe_unit()
        if h == 1 and not final:
            mini_ln(ln_idx)
        ln_chain([main_ln_item(ln_idx, t, final) for t in tiles])

    def save_halo():
        S.op("dve", lambda e: e.tensor_copy(out=hsave[:, :, :], in_=hb[:, :, HT - HALO:HT]),
             reads=[("hb", 2)] + [("hbc", c, 0) for c in range(NCH)], writes=["hsave"])

    scale_todo = [(t, c) for t in range(NT) for c in range(NCH)]

    def scale_hr(n):
        for _ in range(n):
            if not scale_todo:
                return
            t, c = scale_todo.pop(0)
            a, b = t * TT, (t + 1) * TT
            S.op("dve", lambda e, c=c, a=a, b=b: e.tensor_scalar(out=hr[:, c, a:b], in0=hr[:, c, a:b], scalar1=ALPHA,
                                                                  scalar2=pcol("b_out_ab", c), op0=ALU.mult, op1=ALU.add),
                 reads=[("hr", c, t), "PT"], writes=[("hr", c, t)])

    def stage_m0(h, ph):
        lo, hi = h * HT, (h + 1) * HT
        elo, ehi = max(0, lo - HALO), min(T, hi + HALO)
        col = lambda tok: tok - lo + HALO
        itiles = tile_ranges(elo, ehi, 3)
        otiles = [(t * TT, (t + 1) * TT, t) for t in range(h * 3, h * 3 + 3)]
        akeys = lambda j: [("abuf", j, x) for x in (0, 1, 2, "pad")]
        ukeys = lambda j: [("ubuf", j, x) for x in (0, 1, 2, "pad")]

        def p_in():
            if h == 0:
                save_halo()
                S.op("pool", lambda e: e.memset(abuf[:, :, 0:HALO], 0.0), writes=[("abuf", j, "pad") for j in range(4)])
                S.op("pool", lambda e: e.memset(ubuf[:, :, 0:HALO], 0.0), writes=[("ubuf", j, "pad") for j in range(4)])
            else:
                S.op("pool", lambda e: e.memset(abuf[:, :, AW - HALO:AW], 0.0), writes=[("abuf", j, "pad") for j in range(4)])
                S.op("pool", lambda e: e.memset(ubuf[:, :, AW - HALO:AW], 0.0), writes=[("ubuf", j, "pad") for j in range(4)])
            for j in range(4):
                wslot, wk = a_group(GA_M0IN + j)
                for ti, (a, b) in enumerate(itiles):
                    n = b - a
                    bg_, kg = bank()
                    bv_, kv = bank()
                    us = (h == 1)
                    S.op("pe", lambda pe, bg_=bg_, a=a, b=b, wslot=wslot: mm_hb(pe, bg_, wslot, 0, a, b, us),
                         reads=[wk] + (["hsave"] if us else []) + hb_keys(a, b, us), writes=[kg])
                    S.op("pe", lambda pe, bv_=bv_, a=a, b=b, wslot=wslot: mm_hb(pe, bv_, wslot, 128, a, b, us),
                         reads=[wk] + (["hsave"] if us else []) + hb_keys(a, b, us), writes=[kv])
                    th, kth = tfslot()
                    vh, kvh = tfslot()
                    S.op("act", lambda e, bg_=bg_, th=th, n=n, j=j: e.activation(out=th[:, 0:n], in_=bg_[:, 0:n], func=AF.Tanh,
                                                                                bias=dcol("hbg", j), scale=0.5),
                         reads=[kg, "DT"], writes=[kth])
                    S.op("act", lambda e, bv_=bv_, vh=vh, n=n, j=j: e.activation(out=vh[:, 0:n], in_=bv_[:, 0:n], func=AF.Identity,
                                                                                bias=dcol("hbv", j), scale=0.5),
                         reads=[kv, "DT"], writes=[kvh])
                    S.op("dve", lambda e, th=th, vh=vh, n=n, j=j, a=a: e.scalar_tensor_tensor(
                        out=abuf[:, j, col(a):col(a) + n], in0=th[:, 0:n], scalar=1.0, in1=vh[:, 0:n], op0=ALU.add, op1=ALU.mult),
                        reads=[kth, kvh], writes=[("abuf", j, ti)])
                    S.pe_unit()
                    if h == 1:
                        scale_hr(3)
            for i in range(2):
                wslot, wk = a_group(GA_M0IN + 4 + i)
                for ti, (a, b) in enumerate(itiles):
                    n = b - a
                    for jl in range(2):
                        j = 2 * i + jl
                        bu, ku = bank()
                        us = (h == 1)
                        S.op("pe", lambda pe, bu=bu, a=a, b=b, wslot=wslot, jl=jl: mm_hb(pe, bu, wslot, jl * 128, a, b, us),
                             reads=[wk] + (["hsave"] if us else []) + hb_keys(a, b, us), writes=[ku])
                        S.op("act", lambda e, bu=bu, n=n, j=j, a=a: e.activation(out=ubuf[:, j, col(a):col(a) + n], in_=bu[:, 0:n],
                                                                                func=AF.Identity, bias=pcol("b_in_ab", 8 + j), scale=1.0),
                             reads=[ku, "PT"], writes=[("ubuf", j, ti)])
                    S.pe_unit()
                    if h == 1:
                        scale_hr(3)
            build31(1)

        def dsel(j):
            return (d31a, [("catmem",)]) if j % 2 == 0 else (d31b, [("d31b",)])

        def build31(j):
            dbuf, dkeys = dsel(j)
            for k in range(31):
                build_diag(dbuf, k, pcol("conv_a_w", k * 4 + j), dkeys)

        def p_conv():
            order = [1, 0, 3, 2]
            u = 0
            pend2 = [None]
            for idx, j in enumerate(order):
                dbuf, dkeys = dsel(j)
                if idx + 1 < 4:
                    build31(order[idx + 1])
                for (a, b, t) in otiles:
                    bc, kc_ = bank()

                    def conv(pe, bc=bc, a=a, j=j, dbuf=dbuf):
                        inst = None
                        for k in range(31):
                            c0 = a - lo + k
                            inst = pe.matmul(bc[:, 0:TT], dbuf[:, k, :], abuf[:, j, c0:c0 + TT], start=(k == 0), stop=(k == 30))
                        return inst
                    S.op("pe", conv, reads=list(dkeys) + akeys(j), writes=[kc_])
                    S.op("act", lambda e, bc=bc, j=j, a=a: e.activation(out=ac32[:, j, a - lo:a - lo + TT], in_=bc[:, 0:TT],
                                                                        func=AF.Identity, bias=pcol("conv_a_b", j), scale=1.0),
                         reads=[kc_, "PT"], writes=[("ac32", j, t)] + RB_KEYS)
                    S.pe_unit()
                    pa, pb_, pt = otiles[u // 4]
                    p2 = pool_unit(pa, pb_, pt, u % 4)
                    if pend2[0] is not None:
                        pend2[0]()
                    pend2[0] = p2
                    u += 1
            pend2[0]()

        def pool_unit(a, b, t, g):
            if True:
                if True:
                    wd = 2 ** (g + 1)
                    bA, kA = bank()

                    def taps(pe, bA=bA, g=g, wd=wd, a=a):
                        inst = None
                        ks = list(range(-(wd // 2), wd // 2))
                        for i, k in enumerate(ks):
                            c0 = col(a) + k
                            lhs = negI[:, g, :] if k == 0 else ident[:, :]
                            inst = pe.matmul(bA[:, 0:TT], lhs, ubuf[:, g, c0:c0 + TT], start=(i == 0), stop=(i == len(ks) - 1))
                        return inst
                    S.op("pe", taps, reads=["ident", "negI"] + ukeys(g), writes=[kA])
                    pb, kp = pbslot()
                    S.op("act", lambda e, bA=bA, pb=pb, wd=wd: e.activation(out=pb[:, 0:TT], in_=bA[:, 0:TT], func=AF.Identity,
                                                                           bias=0.0, scale=1.0 / wd),
                         reads=[kA], writes=[kp])
                    for edge in (0, 1):
                        if (edge == 0 and a != 0) or (edge == 1 and b != T):
                            continue
                        cb_ = g * 32 + 16 * edge
                        e0 = 0 if edge == 0 else TT - 8
                        ei = state["et"]
                        state["et"] = 1 - ei
                        et = etmp[:, ei, :]
                        ke = ("et", ei)
                        S.op("dve", lambda e, bA=bA, e0=e0, cb_=cb_, et=et: e.tensor_tensor(
                            out=et, in0=bA[:, e0:e0 + 8], in1=CT[:, cb_:cb_ + 8], op=ALU.mult),
                            reads=[kA, "CT"], writes=[ke])
                        S.op("dve", lambda e, pb=pb, e0=e0, cb_=cb_, g=g, a=a: e.tensor_tensor(
                            out=pb[:, e0:e0 + 8], in0=ubuf[:, g, col(a) + e0:col(a) + e0 + 8], in1=CT[:, cb_ + 8:cb_ + 16], op=ALU.mult),
                            reads=["CT"] + ukeys(g), writes=[kp])
                        S.op("dve", lambda e, pb=pb, e0=e0, et=et: e.tensor_tensor(
                            out=pb[:, e0:e0 + 8], in0=pb[:, e0:e0 + 8], in1=et, op=ALU.add),
                            reads=[kp, ke], writes=[kp])
                    def part2(pb=pb, kp=kp, g=g, a=a, t=t):
                        bP, kP = bank()
                        S.op("pe", lambda pe: pe.matmul(bP[:, 0:TT], poolw[:, g, :], pb[:, 0:TT], start=True, stop=True),
                             reads=[kp, "poolw"], writes=[kP])
                        S.op("act", lambda e: e.activation(out=cat[:, 4 + g, a - lo:a - lo + TT], in_=bP[:, 0:TT],
                                                             func=AF.Identity, bias=0.0, scale=pcol("pool_scale", g)),
                             reads=[kP, "PT"], writes=[("msrc", 4 + g, t), ("catmem2",)])
                    S.pe_unit()
                    return part2

        def p_lna():
            items = []
            for (a, b, t) in otiles:
                src = lambda c, a=a: ac32[:, c, a - lo:a - lo + TT]
                keys = [("ac32", c, t) for c in range(4)]

                def epi(c, a=a, t=t, src=src, keys=keys):
                    S.op("act", lambda e: e.activation(out=cat[:, c, a - lo:a - lo + TT], in_=src(c), func=AF.Silu,
                                                         bias=pcol("norm_a_b", c), scale=pcol("norm_a_g", c)),
                         reads=[keys[c], "PT"] + RB_KEYS, writes=[("msrc", c, t), ("catmem",)])
                lna = dict(src=src, keys=keys, nch=4, n=TT, ones=ones4, epi=epi, outs=[("msrc", c, t) for c in range(4)],
                           xr=RB_KEYS)
                items.append(lna)
            if h == 0:
                ln_chain(items, base=1, per=7, split0=False)
            else:
                A, B, C = items
                B["alt"] = True
                C["alt"] = 3
                ln_triple(A, B, C)

        def p_out():
            scale_hr(100)
            out_proj(h, cat, GA_M0OUT, 0, final=(n_stages == 1))

        for p in ph:
            {"in": p_in, "conv": p_conv, "lna": p_lna, "out": p_out}[p]()

    def stage_ffn(l, h):
        lo, hi = h * HT, (h + 1) * HT
        if h == 0:
            save_halo()
        tiles = [h * 3 + i for i in range(3)]
        ga0 = GA_F0UP if l == 0 else GA_F1UP
        units = [(j, t) for j in range(NFC) for t in tiles]
        info = {}
        slots = {}

        fb = {"gc": 0, "v": 0}

        def fbank(kind):
            i = fb[kind]
            fb[kind] = (i + 1) % 3
            b = i if kind == "gc" else 3 + i
            return ps[b], ("ps", b)

        def build_fdiag(j):
            for k in range(3):
                build_diag(diag, (j % 2) * 3 + k, pcol("ffn_conv_w", (l * 3 + k) * 22 + j), [("fd", j % 2)])

        def emit_up(j, t):
            if t == tiles[0]:
                slots[j] = a_group(ga0 + j)
            wslot, wk = slots[j]
            a, b = t * TT, (t + 1) * TT
            ea, eb = max(0, a - 1), min(T, b + 1)
            n = eb - ea
            off = ea - (a - 1)
            bG, kG = fbank("gc")
            bV, kV = fbank("v")
            us = (h == 1)
            un = (h == 0 and eb == HT + 1)
            S.op("pe", lambda pe: mm_hb(pe, bG, wslot, 0, ea, eb, us, un),
                 reads=[wk] + (["hsave"] if us else []) + (["hnext"] + hb_keys(ea, HT) if un else hb_keys(ea, eb, us)), writes=[kG])
            S.op("pe", lambda pe: mm_hb(pe, bV, wslot, 128, a, b, False), reads=[wk] + hb_keys(a, b), writes=[kV])
            gb, kgb = gbslot()
            if a == 0:
                S.op("dve", lambda e: e.memset(gb[:, 0:1], 0.0), writes=[kgb])
            if b == T:
                S.op("dve", lambda e: e.memset(gb[:, TT + 1:TT + 2], 0.0), writes=[kgb])
            S.op("act", lambda e: e.activation(out=gb[:, off:off + n], in_=bG[:, 0:n], func=AF.Identity,
                                                 bias=pcol("ffn_b_up", l * 44 + j), scale=1.0),
                 reads=[kG, "PT"], writes=[kgb])
            info[(j, t)] = (bV, kV, gb, kgb)
            S.pe_unit()

        def emit_conv(j, t):
            bV, kV, gb, kgb = info.pop((j, t))
            a = t * TT
            bC, kC = fbank("gc")

            def conv(pe):
                inst = None
                for k in range(3):
                    inst = pe.matmul(bC[:, 0:TT], diag[:, (j % 2) * 3 + k, :], gb[:, k:k + TT], start=(k == 0), stop=(k == 2))
                return inst
            S.op("pe", conv, reads=[kgb, ("fd", j % 2)], writes=[kC])
            sg, ksg = tfslot()
            S.op("act", lambda e: e.activation(out=sg[:, 0:TT], in_=bC[:, 0:TT], func=AF.Silu,
                                                 bias=pcol("ffn_conv_b", l * 22 + j), scale=1.0),
                 reads=[kC, "PT"], writes=[ksg])
            S.op("dve", lambda e: e.scalar_tensor_tensor(out=hid[:, j, a - lo:a - lo + TT], in0=bV[:, 0:TT],
                                                           scalar=pcol("ffn_b_up", l * 44 + 22 + j), in1=sg[:, 0:TT],
                                                           op0=ALU.add, op1=ALU.mult),
                 reads=[kV, ksg, "PT"], writes=[("hid", j, t)])

        if l == 0 and h == 0:
            b_prefetch(RB_SLOTS - 1)
        build_fdiag(0)
        build_fdiag(1)
        prev = None
        for (j, t) in units:
            emit_up(j, t)
            if prev is not None:
                emit_conv(*prev)
                if prev[1] == tiles[-1] and prev[0] + 2 < NFC:
                    build_fdiag(prev[0] + 2)
            prev = (j, t)
        emit_conv(*prev)
        final = (n_stages == 2 * l + 2)
        for oc in range(8):
            wslot, wk = b_slab(l * 8 + oc)
            for t in tiles:
                a, b = t * TT, (t + 1) * TT
                bk, kb = bank()

                def down(pe, bk=bk, a=a, wslot=wslot):
                    inst = None
                    for kc in range(NFC):
                        inst = pe.matmul(bk[:, 0:TT], wslot[:, kc, :], hid[:, kc, a - lo:a - lo + TT],
                                         start=(kc == 0), stop=(kc == NFC - 1))
                    return inst
                S.op("pe", down, reads=[wk] + [("hid", kc, t) for kc in range(NFC)], writes=[kb])
                S.op("dve", lambda e, bk=bk, oc=oc, a=a, b=b: e.tensor_tensor(out=hr[:, oc, a:b], in0=bk[:, 0:TT],
                                                                                in1=hr[:, oc, a:b], op=ALU.add),
                     reads=[kb, ("hr", oc, t)], writes=[("hr", oc, t)])
                S.pe_unit()
        if h == 1 and not final:
            mini_ln(2 * l + 1)
        if final and h == 1:
            A, B, C = [main_ln_item(2 * l + 1, t, final) for t in tiles]
            B["alt"] = 8
            ln_pair(A, B, C)
        else:
            ln_chain([main_ln_item(2 * l + 1, t, final) for t in tiles])

    def stage_m1(h):
        lo, hi = h * HT, (h + 1) * HT
        if h == 0:
            save_halo()
            S.op("pool", lambda e: e.memset(cvb[:, :, 0:1], 0.0),
                 writes=[("cv", j, "pad") for j in range(8)] + [("hid", kc, t) for kc in range(NFC) for t in range(NT)])
        else:
            S.op("pool", lambda e: e.memset(cvb[:, :, CVW - 1:CVW], 0.0), writes=[("cv", j, "pad") for j in range(8)])
        tiles = [h * 3 + i for i in range(3)]
        ccol = lambda tok: tok - lo + 1
        for j in range(8):
            wslot, wk = a_group(GA_M1IN + j)
            for i, t in enumerate(tiles):
                a, b = t * TT, (t + 1) * TT
                ea = max(0, a - 1) if i == 0 else a
                eb = min(T, b + 1) if i == 2 else b
                n = eb - ea
                bc, kc_ = bank()
                bv, kv = bank()
                us = (h == 1)
                un = (h == 0 and eb == HT + 1)
                hk = (["hnext"] + hb_keys(ea, HT)) if un else hb_keys(ea, eb, us)
                S.op("pe", lambda pe, bc=bc, ea=ea, eb=eb, wslot=wslot, un=un: mm_hb(pe, bc, wslot, 0, ea, eb, us, un),
                     reads=[wk] + (["hsave"] if us else []) + hk, writes=[kc_])
                S.op("pe", lambda pe, bv=bv, ea=ea, eb=eb, wslot=wslot, un=un: mm_hb(pe, bv, wslot, 128, ea, eb, us, un),
                     reads=[wk] + (["hsave"] if us else []) + hk, writes=[kv])
                vb, kvb = tfslot()
                S.op("act", lambda e, bv=bv, vb=vb, n=n, j=j: e.activation(out=vb[:, 0:n], in_=bv[:, 0:n], func=AF.Identity,
                                                                            bias=pcol("b_in_c", 16 + j), scale=1.0),
                     reads=[kv, "PT"], writes=[kvb])
                S.op("dve", lambda e, bc=bc, vb=vb, n=n, j=j, ea=ea: e.scalar_tensor_tensor(
                    out=cvb[:, j, ccol(ea):ccol(ea) + n], in0=bc[:, 0:n], scalar=pcol("b_in_c", 8 + j), in1=vb[:, 0:n],
                    op0=ALU.add, op1=ALU.mult),
                    reads=[kc_, kvb, "PT"], writes=[("cv", j, i)])
                S.pe_unit()

        def build_cdiag(j):
            for k in range(3):
                build_diag(diag, (j % 2) * 3 + k, pcol("conv_c_w", k * 8 + j), [("fd", j % 2)])
        build_cdiag(0)
        for ii in range(4):
            wslot, wk = a_group(GA_M1IN + 8 + ii)
            for jl in range(2):
                j = 2 * ii + jl
                if j + 1 < 8:
                    build_cdiag(j + 1)
                for t in tiles:
                    a, b = t * TT, (t + 1) * TT
                    bB, kB = bank()
                    S.op("pe", lambda pe, bB=bB, a=a, b=b, wslot=wslot, jl=jl: mm_hb(pe, bB, wslot, jl * 128, a, b, False),
                         reads=[wk] + hb_keys(a, b), writes=[kB])
                    bC, kC = bank()

                    def conv(pe, bC=bC, j=j, a=a):
                        inst = None
                        for k in range(3):
                            c0 = a - lo + k
                            inst = pe.matmul(bC[:, 0:TT], diag[:, (j % 2) * 3 + k, :], cvb[:, j, c0:c0 + TT],
                                             start=(k == 0), stop=(k == 2))
                        return inst
                    S.op("pe", conv, reads=[("fd", j % 2)] + [("cv", j, x) for x in (0, 1, 2, "pad")], writes=[kC])
                    cs, kcs = tfslot()
                    S.op("act", lambda e, bC=bC, cs=cs, j=j: e.activation(out=cs[:, 0:TT], in_=bC[:, 0:TT], func=AF.Identity,
                                                                          bias=pcol("conv_c_b", j), scale=1.0),
                         reads=[kC, "PT"], writes=[kcs])
                    S.op("dve", lambda e, bB=bB, cs=cs, j=j, a=a: e.scalar_tensor_tensor(
                        out=ybuf[:, j, a - lo:a - lo + TT], in0=bB[:, 0:TT], scalar=pcol("b_in_c", j), in1=cs[:, 0:TT],
                        op0=ALU.add, op1=ALU.mult),
                        reads=[kB, kcs, "PT"], writes=[("msrc", j, t)])
                    S.pe_unit()
        out_proj(h, ybuf, GA_M1OUT, 2, final=(n_stages == 3))

    S.op("sp", lambda e: e.dma_start(out=PT[:, :], in_=pt_d[:, :]), writes=["PT"], dma="s_pt")
    S.op("sp", lambda e: e.dma_start(out=CT[:, :], in_=ctab_d[:, 128:256]), writes=["CT"], dma="s_ct")
    for t in range(NT):
        a, b = t * TT, (t + 1) * TT
        pass
    S.op("pool", lambda e: e.dma_start(out=ident[:, :], in_=ctab_d[:, 0:128]), writes=["ident"], dma="s_id")
    XS = HT + HALO
    for c in range(NCH):
        S.op("pool", lambda e, c=c: e.dma_start(out=hb[:, c, 0:XS], in_=xT[:, c, 0:XS], max_dma_last_dim=4096),
             writes=[("hbc", c, 0)], dma=("s_xb", c))
    a_prefetch(0)
    for c in range(NCH):
        S.op("pool", lambda e, c=c: e.dma_start(out=hb[:, c, XS:T], in_=xT[:, c, XS:T], max_dma_last_dim=4096),
             writes=[("hbc", c, 1)], dma=("s_xb2", c))
        if c == 3:
            a_prefetch(1)
    S.op("pool", lambda e: e.dma_start(out=poolw[:, :, :], in_=poolw_d[:, :, :]), writes=["poolw"], dma="s_pw")
    for t in range(NT):
        a, b = t * TT, (t + 1) * TT
        S.op("sp", lambda e, a=a, b=b: e.dma_start(out=hr[:, :, a:b], in_=xT[:, :, a:b]),
             reads=[("hbc", NCH - 1, 1)], writes=[("hr", c, t) for c in range(NCH)], dma=("s_x", t))
    a_prefetch(3)
    S.op("dve", lambda e: e.memset(ones8[:, :], 1.0 / 1024.0), writes=["ones"])
    S.op("dve", lambda e: e.memset(ones4[:, :], 1.0 / 512.0), writes=["ones"])
    S.op("dve", lambda e: e.memset(DT[:, DCOL["eps"]:DCOL["eps"] + 1], EPS), writes=["DT"])
    for g in range(4):
        S.op("dve", lambda e, g=g: e.tensor_scalar(out=negI[:, g, :], in0=ident[:, :], scalar1=-float(2 ** (g + 1) - 1),
                                                    scalar2=None, op0=ALU.mult),
             reads=["ident"], writes=["negI"])
    cv_ = PCOL["b_in_ab"]
    S.op("dve", lambda e: e.tensor_scalar(out=DT[:, DCOL["hbg"]:DCOL["hbg"] + 4], in0=PT[:, cv_ + 4:cv_ + 8], scalar1=0.5,
                                           scalar2=None, op0=ALU.mult), reads=["PT"], writes=["DT"])
    S.op("dve", lambda e: e.tensor_scalar(out=DT[:, DCOL["hbv"]:DCOL["hbv"] + 4], in0=PT[:, cv_:cv_ + 4], scalar1=0.5,
                                           scalar2=None, op0=ALU.mult), reads=["PT"], writes=["DT"])
    ln_defs = [("mix_ln_g", "mix_ln_b", 0, "ffn_b_down", 0), ("ffn_ln_g", "ffn_ln_b", 0, "b_out_c", 0),
               ("mix_ln_g", "mix_ln_b", 8, "ffn_b_down", 8)]
    for i, (gn, bn, off, nn, noff) in enumerate(ln_defs):
        g0, b0, n0 = PCOL[gn] + off, PCOL[bn] + off, PCOL[nn] + noff
        S.op("dve", lambda e, i=i, g0=g0: e.tensor_scalar(out=DT[:, DCOL["ag"] + i * 8:DCOL["ag"] + i * 8 + 8], in0=PT[:, g0:g0 + 8],
                                                          scalar1=ALPHA, scalar2=None, op0=ALU.mult), reads=["PT"], writes=["DT"])
        S.op("dve", lambda e, i=i, b0=b0, n0=n0: e.scalar_tensor_tensor(out=DT[:, DCOL["cc"] + i * 8:DCOL["cc"] + i * 8 + 8],
                                                                        in0=PT[:, b0:b0 + 8], scalar=ALPHA, in1=PT[:, n0:n0 + 8],
                                                                        op0=ALU.mult, op1=ALU.add), reads=["PT"], writes=["DT"])
    stage_m0(0, ["in", "conv", "lna"])
    stage_m0(1, ["in"])
    stage_m0(0, ["out"])
    stage_m0(1, ["conv", "lna", "out"])
    if n_stages >= 2:
        stage_ffn(0, 0)
        stage_ffn(0, 1)
    if n_stages >= 3:
        stage_m1(0)
        stage_m1(1)
    if n_stages >= 4:
        stage_ffn(1, 0)
        stage_ffn(1, 1)
    S.flush()
    if max_ops is not None:
        del S.ops[max_ops:]
    n_out = sum(1 for o in S.ops if o.dma == "osem")
    if n_out:
        S.op("sp", lambda e: e.wait_ge(SEMS["osem"], 16 * n_out), reads=[], writes=[])

    S.analyze()
    SEMS = {}
    for k in S.sem_keys():
        SEMS[k] = es.enter_context(nc.semaphore("s_" + "_".join(str(x) for x in (k if isinstance(k, tuple) else (k,)))))
    with nc.Block() as block:
        @block.tensor
        def _(e):
            S.emit("pe", e, SEMS)

        @block.scalar
        def _(e):
            S.emit("act", e, SEMS)

        @block.vector
        def _(e):
            S.emit("dve", e, SEMS)

        @block.gpsimd
        def _(e):
            S.emit("pool", e, SEMS)

        @block.sync
        def _(e):
            S.emit("sp", e, SEMS)
    es.close()
    return nc


_CACHE = {}


def kernel(**inputs):
    n_stages = int(inputs.pop("_n_stages", 4))
    max_ops = inputs.pop("_max_ops", None)
    shared, xts = host_layout(inputs)
    if (n_stages, max_ops) not in _CACHE:
        _CACHE[(n_stages, max_ops)] = build(n_stages, max_ops)
    nc = _CACHE[(n_stages, max_ops)]
    in_maps = [dict(shared, xT=xts[b]) for b in range(8)]
    res = run_bass_kernel_spmd(nc, in_maps, core_ids=list(range(8)))
    outs = []
    for b in range(8):
        o = np.asarray(res.results[b]["outT"], np.float32)
        outs.append(o.transpose(1, 0, 2).reshape(D, SEQ).T)
    return np.ascontiguousarray(np.stack(outs, 0).astype(np.float32))
```
